# Optimizing a Trainium2 kernel written in Bass

```python
import math
import jax
import jax.numpy as jnp
from jax import lax
import numpy as np

D_MODEL = 1024
BATCH = 8
SEQ = 2048
DEPTH = 2

GRID_W = 64
CTX_LEN = 256
HEAD_DIM = 64
DIFF_HEADS = 4
DIFF_V_DIM = 2 * HEAD_DIM
NA_HEADS = 8
NA_KH = 8
NA_KW = 16
Q_BLOCK = 128
ROPE_BASE = 10000.0
ROPE_AXIS_DIM = HEAD_DIM // 2
SSM_GROUP = 16
SSM_GROUPS = D_MODEL // SSM_GROUP
SSM_STATE = 64
MLP_HIDDEN = 4 * D_MODEL
N_EVEN = (DEPTH + 1) // 2
N_ODD = DEPTH // 2
A_QK = DIFF_HEADS * 2 * HEAD_DIM
A_V = DIFF_HEADS * DIFF_V_DIM
B_QKV = NA_HEADS * HEAD_DIM
IN_PROJ = 2 * A_QK + A_V + 3 * B_QKV
MIX_OUT = A_V + B_QKV
NORM_EPS = 1e-6
SUBLN_EPS = 1e-5

kernel_name = 'hybrid_diffattn_natten_s5_block'


def _rmsnorm(x, g, eps):
    xf = x.astype(jnp.float32)
    y = xf * lax.rsqrt(jnp.mean(xf * xf, axis=-1, keepdims=True) + eps)
    return (y * g.astype(jnp.float32)).astype(x.dtype)


def _rotate(x, cos, sin):
    half = x.shape[-1] // 2
    x1, x2 = x[..., :half], x[..., half:]
    cos = cos.astype(x.dtype)
    sin = sin.astype(x.dtype)
    return jnp.concatenate([x1 * cos - x2 * sin, x2 * cos + x1 * sin], axis=-1)


def _rope2d(x, cos_r, sin_r, cos_c, sin_c):
    return jnp.concatenate([_rotate(x[..., :ROPE_AXIS_DIM], cos_r, sin_r),
                            _rotate(x[..., ROPE_AXIS_DIM:], cos_c, sin_c)], axis=-1)


def _rope_tables(seq):
    t = jnp.arange(seq)
    row = (t // GRID_W).astype(jnp.float32)
    col = (t % GRID_W).astype(jnp.float32)
    n_freq = ROPE_AXIS_DIM // 2
    inv_freq = ROPE_BASE ** (-jnp.arange(n_freq, dtype=jnp.float32) / n_freq)
    ang_r = (row[:, None] * inv_freq[None, :]).reshape(seq, 1, 1, n_freq)
    ang_c = (col[:, None] * inv_freq[None, :]).reshape(seq, 1, 1, n_freq)
    return (jnp.cos(ang_r), jnp.sin(ang_r), jnp.cos(ang_c), jnp.sin(ang_c))


def _split_proj(p):
    b, n, _ = p.shape
    qa = p[..., 0:A_QK].reshape(b, n, DIFF_HEADS, 2, HEAD_DIM)
    ka = p[..., A_QK:2 * A_QK].reshape(b, n, DIFF_HEADS, 2, HEAD_DIM)
    va = p[..., 2 * A_QK:2 * A_QK + A_V].reshape(b, n, DIFF_HEADS, DIFF_V_DIM)
    base = 2 * A_QK + A_V
    qn = p[..., base:base + B_QKV].reshape(b, n, NA_HEADS, HEAD_DIM)
    kn = p[..., base + B_QKV:base + 2 * B_QKV].reshape(b, n, NA_HEADS, HEAD_DIM)
    vn = p[..., base + 2 * B_QKV:base + 3 * B_QKV].reshape(b, n, NA_HEADS, HEAD_DIM)
    return qa, ka, va, qn, kn, vn


def _diff_attend(q, k, v, lam, lam_init, g):
    s = jnp.einsum('bqhmd,bkhmd->bhmqk', q, k).astype(jnp.float32) * (HEAD_DIM ** -0.5)
    p = jax.nn.softmax(s, axis=-1)
    a = p[:, :, 0] - lam * p[:, :, 1]
    o = jnp.einsum('bhqk,bkhe->bqhe', a.astype(v.dtype), v)
    return _rmsnorm(o, g, SUBLN_EPS) * (1.0 - lam_init)


def _dense_attend(q, k, v):
    s = jnp.einsum('bqhd,bkhd->bhqk', q, k).astype(jnp.float32) * (HEAD_DIM ** -0.5)
    p = jax.nn.softmax(s, axis=-1)
    return jnp.einsum('bhqk,bkhd->bqhd', p.astype(v.dtype), v)


def _na_latent(qn, kn, vn, kc, vc, rpb, rows):
    b, s, h, d = qn.shape

    def grid(t):
        return t.reshape(b, rows, GRID_W, h, d).transpose(0, 3, 1, 2, 4)

    qg, kg, vg = grid(qn), grid(kn), grid(vn)
    kc_h = kc.transpose(0, 2, 1, 3)
    vc_h = vc.transpose(0, 2, 1, 3)
    kh = min(NA_KH, rows)
    kw = NA_KW
    col_q = jnp.arange(GRID_W)
    col_start = jnp.clip(col_q - kw // 2, 0, GRID_W - kw)
    col_idx = col_start[:, None] + jnp.arange(kw)[None, :]
    col_off = col_idx - col_q[:, None] + (NA_KW - 1)
    rpb_cols = rpb[:, :, col_off]
    scale = HEAD_DIM ** -0.5
    n_win = kh * kw

    def row_block(r):
        r_start = jnp.clip(r - kh // 2, 0, rows - kh)
        row_off = r_start + jnp.arange(kh) - r + (NA_KH - 1)
        q_r = lax.dynamic_index_in_dim(qg, r, axis=2, keepdims=False)
        k_band = lax.dynamic_slice_in_dim(kg, r_start, kh, axis=2)
        v_band = lax.dynamic_slice_in_dim(vg, r_start, kh, axis=2)
        k_win = k_band[:, :, :, col_idx]
        v_win = v_band[:, :, :, col_idx]
        bias = jnp.take(rpb_cols, row_off, axis=1).transpose(0, 2, 1, 3)
        s_win = jnp.einsum('bhqd,bhrqcd->bhqrc', q_r, k_win).astype(jnp.float32) * scale + bias.astype(jnp.float32)[None]
        s_ctx = jnp.einsum('bhqd,bhkd->bhqk', q_r, kc_h).astype(jnp.float32) * scale
        p = jax.nn.softmax(jnp.concatenate([s_win.reshape(b, h, GRID_W, n_win), s_ctx], axis=-1), axis=-1)
        p_win = p[..., :n_win].reshape(b, h, GRID_W, kh, kw).astype(vg.dtype)
        p_ctx = p[..., n_win:].astype(vg.dtype)
        return (jnp.einsum('bhqrc,bhrqcd->bhqd', p_win, v_win)
                + jnp.einsum('bhqk,bhkd->bhqd', p_ctx, vc_h))

    o = lax.map(row_block, jnp.arange(rows))
    return o.transpose(1, 0, 3, 2, 4).reshape(b, s, h * d)


def _even_mixer(h, hc, w_in_e, w_out_e, lq1, lk1, lq2, lk2, g_sub, rpb, rope, lam_init, rows, need_ctx):
    b, s, _ = h.shape
    lc = hc.shape[1]
    qa, ka, va, qn, kn, vn = _split_proj(h @ w_in_e)
    qac, kac, vac, qnc, knc, vnc = _split_proj(hc @ w_in_e)
    qa = _rope2d(qa, *rope)
    ka = _rope2d(ka, *rope)
    lam = (jnp.exp(jnp.sum(lq1.astype(jnp.float32) * lk1.astype(jnp.float32)))
           - jnp.exp(jnp.sum(lq2.astype(jnp.float32) * lk2.astype(jnp.float32))) + lam_init)
    k_all = jnp.concatenate([ka, kac], axis=1)
    v_all = jnp.concatenate([va, vac], axis=1)
    nb = s // Q_BLOCK
    qb = qa.reshape(b, nb, Q_BLOCK, DIFF_HEADS, 2, HEAD_DIM).swapaxes(0, 1)
    oa = lax.map(lambda qq: _diff_attend(qq, k_all, v_all, lam, lam_init, g_sub), qb)
    oa = oa.swapaxes(0, 1).reshape(b, s, A_V)
    ob = _na_latent(qn, kn, vn, knc, vnc, rpb, rows)
    y = jnp.concatenate([oa, ob], axis=-1) @ w_out_e
    yc = None
    if need_ctx:
        oac = _diff_attend(qac, kac, vac, lam, lam_init, g_sub).reshape(b, lc, A_V)
        obc = _dense_attend(qnc, knc, vnc).reshape(b, lc, B_QKV)
        yc = jnp.concatenate([oac, obc], axis=-1) @ w_out_e
    return y, yc


def _zoh(lam_re, lam_im, log_step, b_re, b_im):
    lr = lam_re.astype(jnp.float32)
    li = lam_im.astype(jnp.float32)
    delta = jnp.exp(log_step.astype(jnp.float32))[:, None]
    mag = jnp.exp(lr * delta)
    ar = mag * jnp.cos(li * delta)
    ai = mag * jnp.sin(li * delta)
    den = lr * lr + li * li
    nr = ar - 1.0
    cr = (nr * lr + ai * li) / den
    ci = (ai * lr - nr * li) / den
    br = b_re.astype(jnp.float32)
    bi = b_im.astype(jnp.float32)
    bbr = cr[..., None] * br - ci[..., None] * bi
    bbi = cr[..., None] * bi + ci[..., None] * br
    return ar, ai, bbr, bbi


def _complex_affine_combine(e1, e2):
    a1r, a1i, b1r, b1i = e1
    a2r, a2i, b2r, b2i = e2
    return (a2r * a1r - a2i * a1i,
            a2r * a1i + a2i * a1r,
            a2r * b1r - a2i * b1i + b2r,
            a2r * b1i + a2i * b1r + b2i)


def _ssm_scan(u, ar, ai, bbr, bbi, s0, reverse):
    bur = jnp.einsum('blgh,gph->blgp', u, bbr)
    bui = jnp.einsum('blgh,gph->blgp', u, bbi)
    a_r = jnp.broadcast_to(ar, bur.shape)
    a_i = jnp.broadcast_to(ai, bur.shape)
    acc_r, acc_i, sr, si = lax.associative_scan(_complex_affine_combine, (a_r, a_i, bur, bui), reverse=reverse, axis=1)
    if s0 is not None:
        s0r = s0[0][:, None]
        s0i = s0[1][:, None]
        sr = sr + acc_r * s0r - acc_i * s0i
        si = si + acc_r * s0i + acc_i * s0r
    return sr, si


def _readout(sr, si, c_re, c_im):
    return (jnp.einsum('blgp,ghp->blgh', sr, c_re.astype(jnp.float32))
            - jnp.einsum('blgp,ghp->blgh', si, c_im.astype(jnp.float32)))


def _s5_mixer(h, hc, lam_re, lam_im, log_step, b_re, b_im, c_re, c_im, d_skip, w_a, w_b, need_ctx):
    b, s, dm = h.shape
    lc = hc.shape[1]
    u = h.astype(jnp.float32).reshape(b, s, SSM_GROUPS, SSM_GROUP)
    uc = hc.astype(jnp.float32).reshape(b, lc, SSM_GROUPS, SSM_GROUP)
    y = jnp.zeros(u.shape, jnp.float32)
    yc = jnp.zeros(uc.shape, jnp.float32)
    for dr in range(2):
        rev = dr == 1
        ar, ai, bbr, bbi = _zoh(lam_re[dr], lam_im[dr], log_step[dr], b_re[dr], b_im[dr])
        scr, sci = _ssm_scan(uc, ar, ai, bbr, bbi, None, rev)
        last = 0 if rev else lc - 1
        sr, si = _ssm_scan(u, ar, ai, bbr, bbi, (scr[:, last], sci[:, last]), rev)
        y = y + _readout(sr, si, c_re[dr], c_im[dr])
        if need_ctx:
            yc = yc + _readout(scr, sci, c_re[dr], c_im[dr])

    def finish(yy, hh, n):
        z = yy.reshape(b, n, dm) + d_skip.astype(jnp.float32) * hh.astype(jnp.float32)
        z = jax.nn.gelu(z)
        out = (z @ w_a.astype(jnp.float32)) * jax.nn.sigmoid(z @ w_b.astype(jnp.float32))
        return out.astype(h.dtype)

    y_out = finish(y, h, s)
    yc_out = finish(yc, hc, lc) if need_ctx else None
    return y_out, yc_out


def _mlp(h, w1, w2):
    return jnp.square(jax.nn.relu(h @ w1)) @ w2


def setup_inputs(seed: int = 0) -> dict:
    key = jax.random.key(seed)
    ks = jax.random.split(key, 32)
    f32 = jnp.float32
    d = D_MODEL

    def nrm(k, shape, std):
        return jax.random.normal(k, shape, f32) * std

    return {
        'x': nrm(ks[0], (BATCH, SEQ, d), 1.0),
        'c': nrm(ks[1], (BATCH, d), 1.0),
        'ctx': nrm(ks[2], (BATCH, CTX_LEN, d), 1.0),
        'c_ctx': nrm(ks[3], (d,), 1.0),
        'w_ada': nrm(ks[4], (DEPTH, d, 6 * d), d ** -0.5),
        'b_ada': nrm(ks[5], (DEPTH, 6 * d), 0.02),
        'norm1_g': 1.0 + nrm(ks[6], (DEPTH, d), 0.02),
        'norm2_g': 1.0 + nrm(ks[7], (DEPTH, d), 0.02),
        'final_g': 1.0 + nrm(ks[8], (d,), 0.02),
        'w_in': nrm(ks[9], (N_EVEN, d, IN_PROJ), d ** -0.5),
        'w_out': nrm(ks[10], (N_EVEN, MIX_OUT, d), MIX_OUT ** -0.5),
        'lam_q1': nrm(ks[11], (N_EVEN, HEAD_DIM), 0.1),
        'lam_k1': nrm(ks[12], (N_EVEN, HEAD_DIM), 0.1),
        'lam_q2': nrm(ks[13], (N_EVEN, HEAD_DIM), 0.1),
        'lam_k2': nrm(ks[14], (N_EVEN, HEAD_DIM), 0.1),
        'subln_g': 1.0 + nrm(ks[15], (N_EVEN, DIFF_V_DIM), 0.02),
        'na_rpb': nrm(ks[16], (N_EVEN, NA_HEADS, 2 * NA_KH - 1, 2 * NA_KW - 1), 0.02),
        'ssm_lam_re': -0.5 + nrm(ks[17], (N_ODD, 2, SSM_GROUPS, SSM_STATE), 0.01),
        'ssm_lam_im': (math.pi * jnp.arange(SSM_STATE, dtype=f32))[None, None, None, :]
                      + nrm(ks[18], (N_ODD, 2, SSM_GROUPS, SSM_STATE), 0.01),
        'ssm_log_step': jax.random.uniform(ks[19], (N_ODD, 2, SSM_GROUPS), f32,
                                           minval=math.log(1e-3), maxval=math.log(1e-1)),
        'ssm_b_re': nrm(ks[20], (N_ODD, 2, SSM_GROUPS, SSM_STATE, SSM_GROUP), (2 * SSM_GROUP) ** -0.5),
        'ssm_b_im': nrm(ks[21], (N_ODD, 2, SSM_GROUPS, SSM_STATE, SSM_GROUP), (2 * SSM_GROUP) ** -0.5),
        'ssm_c_re': nrm(ks[22], (N_ODD, 2, SSM_GROUPS, SSM_GROUP, SSM_STATE), (2 * SSM_STATE) ** -0.5),
        'ssm_c_im': nrm(ks[23], (N_ODD, 2, SSM_GROUPS, SSM_GROUP, SSM_STATE), (2 * SSM_STATE) ** -0.5),
        'ssm_d': nrm(ks[24], (N_ODD, d), 1.0),
        'glu_w_a': nrm(ks[25], (N_ODD, d, d), d ** -0.5),
        'glu_w_b': nrm(ks[26], (N_ODD, d, d), d ** -0.5),
        'mlp_w1': nrm(ks[27], (DEPTH, d, MLP_HIDDEN), d ** -0.5),
        'mlp_w2': nrm(ks[28], (DEPTH, MLP_HIDDEN, d), MLP_HIDDEN ** -0.5),
    }


def reference(x, c, ctx, c_ctx, w_ada, b_ada, norm1_g, norm2_g, final_g, w_in, w_out,
              lam_q1, lam_k1, lam_q2, lam_k2, subln_g, na_rpb, ssm_lam_re, ssm_lam_im,
              ssm_log_step, ssm_b_re, ssm_b_im, ssm_c_re, ssm_c_im, ssm_d, glu_w_a, glu_w_b,
              mlp_w1, mlp_w2):
    s = x.shape[1]
    rows = s // GRID_W
    rope = _rope_tables(s)
    act_lat = jax.nn.silu(c)
    act_ctx = jax.nn.silu(c_ctx)
    for i in range(DEPTH):
        need_ctx = i < DEPTH - 1
        m = act_lat @ w_ada[i] + b_ada[i]
        mc = act_ctx @ w_ada[i] + b_ada[i]
        sh1, sc1, g1, sh2, sc2, g2 = jnp.split(m[:, None, :], 6, axis=-1)
        sh1c, sc1c, g1c, sh2c, sc2c, g2c = jnp.split(mc, 6, axis=-1)
        h = _rmsnorm(x, norm1_g[i], NORM_EPS) * (1.0 + sc1) + sh1
        hc = _rmsnorm(ctx, norm1_g[i], NORM_EPS) * (1.0 + sc1c) + sh1c
        if i % 2 == 0:
            e = i // 2
            lam_init = 0.8 - 0.6 * math.exp(-0.3 * i)
            y, yc = _even_mixer(h, hc, w_in[e], w_out[e], lam_q1[e], lam_k1[e], lam_q2[e], lam_k2[e],
                                subln_g[e], na_rpb[e], rope, lam_init, rows, need_ctx)
        else:
            o = i // 2
            y, yc = _s5_mixer(h, hc, ssm_lam_re[o], ssm_lam_im[o], ssm_log_step[o], ssm_b_re[o], ssm_b_im[o],
                              ssm_c_re[o], ssm_c_im[o], ssm_d[o], glu_w_a[o], glu_w_b[o], need_ctx)
        x = x + g1 * y
        h2 = _rmsnorm(x, norm2_g[i], NORM_EPS) * (1.0 + sc2) + sh2
        x = x + g2 * _mlp(h2, mlp_w1[i], mlp_w2[i])
        if need_ctx:
            ctx = ctx + g1c * yc
            h2c = _rmsnorm(ctx, norm2_g[i], NORM_EPS) * (1.0 + sc2c) + sh2c
            ctx = ctx + g2c * _mlp(h2c, mlp_w1[i], mlp_w2[i])
    return _rmsnorm(x, final_g, NORM_EPS)
```

```python
import math
import numpy as np
from contextlib import ExitStack
import concourse.bass as bass
import concourse.mybir as mybir
from concourse.bass_utils import run_bass_kernel_spmd

F32 = mybir.dt.float32
BF16 = mybir.dt.bfloat16
I32 = mybir.dt.int32
ALU = mybir.AluOpType
AF = mybir.ActivationFunctionType

ENGS = ("pe", "act", "dve", "pool", "sp")
NDMASEM = 24
TWO_PI = 2.0 * math.pi


class Op:
    __slots__ = ("eng", "fn", "deps", "signal", "sigval", "idx", "is_dma", "sem", "target")

    def __init__(self, eng, fn, is_dma=False):
        self.eng = eng
        self.fn = fn
        self.deps = []
        self.signal = False
        self.sigval = 0
        self.idx = -1
        self.is_dma = is_dma
        self.sem = None
        self.target = 0


class Prog:
    def __init__(self, nc):
        self.nc = nc
        self.ops = {e: [] for e in ENGS}
        self.last_w = {}
        self.readers = {}
        self.waited = {e: {} for e in ENGS}
        self.waited_dma = {e: {} for e in ENGS}
        self.dma_count = 0
        self.dma_last = [None] * NDMASEM
        self.dma_tot = [0] * NDMASEM
        self.n_usem = 0
        self.dve_nop = None

    def op(self, eng, fn, r=(), w=(), is_dma=False):
        o = Op(eng, fn, is_dma)
        r = list(r) + ["PH"]
        deps = []
        for k in r:
            lw = self.last_w.get(k)
            if lw is not None:
                deps.append(lw)
        for k in w:
            lw = self.last_w.get(k)
            if lw is not None:
                deps.append(lw)
            deps.extend(self.readers.get(k, ()))
        if is_dma and eng == "pool" and UNIQUE_POOL_SEMS:
            o.sem = NDMASEM + self.n_usem
            self.n_usem += 1
            o.target = 16
        elif is_dma:
            k = self.dma_count % NDMASEM
            self.dma_count += 1
            prev = self.dma_last[k]
            if prev is not None:
                deps.append(prev)
            self.dma_tot[k] += 16
            o.sem = k
            o.target = self.dma_tot[k]
            self.dma_last[k] = o
        best = {}
        need_nop = False
        for d in deps:
            if d is o:
                continue
            if d.is_dma:
                if self.waited_dma[eng].get(d.sem, 0) >= d.target:
                    continue
                key = ("d", d.sem)
                if key not in best or best[key].target < d.target:
                    best[key] = d
            else:
                if d.eng == "pe" and eng == "pe":
                    continue
                if d.eng == "dve" and eng == "dve" and self.dve_nop is not None:
                    if d.idx == len(self.ops[eng]) - 1:
                        need_nop = True
                    continue
                if self.waited[eng].get(d.eng, -1) >= d.idx:
                    continue
                key = ("c", d.eng)
                if key not in best or best[key].idx < d.idx:
                    best[key] = d
        for key, d in best.items():
            if d.is_dma:
                self.waited_dma[eng][d.sem] = d.target
            else:
                self.waited[eng][d.eng] = d.idx
                d.signal = True
            o.deps.append(d)
        if need_nop:
            nop = Op(eng, self.dve_nop, False)
            nop.idx = len(self.ops[eng])
            self.ops[eng].append(nop)
        o.idx = len(self.ops[eng])
        self.ops[eng].append(o)
        for k in r:
            self.readers.setdefault(k, []).append(o)
        for k in w:
            self.last_w[k] = o
            self.readers[k] = []
        return o

    def dma(self, out, in_, r=(), w=(), eng="sp"):
        return self.op(eng, lambda e: e.dma_start(out=out, in_=in_), r=r, w=w, is_dma=True)

    def emit(self, final_wait_ops=()):
        nc = self.nc
        with ExitStack() as es:
            esem = {e: es.enter_context(nc.semaphore("s_" + e)) for e in ENGS}
            dsem = [es.enter_context(nc.semaphore("d_%d" % i)) for i in range(NDMASEM + self.n_usem)]
            total = {}
            for e in ENGS:
                comp = [o for o in self.ops[e] if not o.is_dma]
                if comp:
                    comp[-1].signal = True
                c = 0
                for o in self.ops[e]:
                    if o.is_dma:
                        continue
                    if o.signal:
                        c += 1
                    o.sigval = c
                total[e] = c
            block = es.enter_context(nc.Block())
            engmap = {"pe": block.tensor, "act": block.scalar, "dve": block.vector,
                      "pool": block.gpsimd, "sp": block.sync}

            def make(e):
                def body(eng):
                    for o in self.ops[e]:
                        for d in o.deps:
                            if d.is_dma:
                                eng.wait_ge(dsem[d.sem], d.target)
                            else:
                                eng.wait_ge(esem[d.eng], d.sigval)
                        ins = o.fn(eng)
                        if o.is_dma:
                            ins.then_inc(dsem[o.sem], 16)
                        elif o.signal:
                            ins.then_inc(esem[e], 1)
                    for f in ENGS:
                        if f != e and total[f] > 0:
                            eng.wait_ge(esem[f], total[f])
                    if e == "sp":
                        for k in range(NDMASEM):
                            if self.dma_tot[k] > 0:
                                eng.wait_ge(dsem[k], self.dma_tot[k])
                return body

            for e in ENGS:
                if self.ops[e] or e == "sp":
                    engmap[e](make(e))


D = 1024
NT = 8
NLAT = 2048
NCTX = 256
NTOK = NLAT + NCTX
CH = [(0, 512), (512, 512), (1024, 512), (1536, 512), (2048, 256)]
NKB = 18
HID_PIECE = 256
NPIECE = 4096 // HID_PIECE
NBT = 21
_STOP = None
UNIQUE_POOL_SEMS = False


def na_kbs(j):
    if j < 2:
        return [0, 1, 2, 3]
    if j > 13:
        return [12, 13, 14, 15]
    return [j - 2, j - 1, j, j + 1, j + 2]


def na_tile(j, kb):
    if j < 2:
        return 5 + j * 4 + kb
    if j > 13:
        return 13 + (j - 14) * 4 + (kb - 12)
    return kb - j + 2


def build(n_layers=2, dbg=False):
    nc = bass.Bass("TRN2", target_bir_lowering=False)
    P = Prog(nc)

    def din(name, shape, dt=F32):
        return nc.dram_tensor(name, list(shape), dt, kind="ExternalInput").ap()

    xT = din("xT", [D, NTOK])
    cc_d = din("cc", [128, 16])
    w_ada = din("w_ada", [2, D, 6 * D])
    b_ada_d = din("b_ada", [128, 96])
    ng_d = din("ng", [128, 40])
    w_inA = din("w_inA", [4, D, 640])
    w_inB = din("w_inB", [4, D, 384])
    w_out = din("w_out", [D, D])
    lamv_d = din("lamv", [128, 256])
    subg_d = din("subg", [128, 128])
    bias_d = din("biasT", [4, 128, 2 * NBT * 128])
    rope_d = din("rope", [128, 2 * NTOK])
    ident_d = din("ident", [128, 128])
    iota_d = din("iota1", [128, 512])
    s5p_d = din("s5p", [2, 128, 96])
    s5b_d = din("s5b", [2, 128, 1024])
    s5c_d = din("s5c", [2, 128, 1024])
    ssmd_d = din("ssmd", [128, 8])
    glu_a = din("glu_w_a", [D, D])
    glu_b = din("glu_w_b", [D, D])
    mlp_w1 = din("mlp_w1", [2, D, 4 * D])
    mlp_w2 = din("mlp_w2", [2, 4 * D, D])
    s5tab = nc.dram_tensor("s5tab", [NT, 128, 4096], BF16).ap()
    n_out_tok = NTOK if dbg else NLAT
    outT = nc.dram_tensor("outT", [D, n_out_tok], F32, kind="ExternalOutput").ap()

    with ExitStack() as es:
        def sb(name, shape, dt):
            return es.enter_context(nc.sbuf_tensor(name, list(shape), dt))

        def psum(name, shape, dt):
            return es.enter_context(nc.psum_tensor(name, list(shape), dt))

        X = sb("X", [128, NT, NTOK], F32)
        H = sb("H", [128, NT, 2560], BF16)
        WS = sb("WS", [128, 8192], BF16)
        WAD = sb("WAD", [128, 2, NT, 128], F32)
        CC = sb("CC", [128, 16], F32)
        SC = sb("SC", [128, NT, 2], F32)
        MOD = sb("MOD", [128, 2, 48, 2], F32)
        BADA = sb("BADA", [128, 96], F32)
        NG = sb("NG", [128, 40], F32)
        AMOD = sb("AMOD", [128, 2, 2, NT, 2], F32)
        ONESB = sb("ONESB", [128, 128], BF16)
        IDB = sb("IDB", [128, 128], BF16)
        LAMV = sb("LAMV", [128, 256], F32)
        LSM = sb("LSM", [128, 16], F32)
        G08 = sb("G08", [128, 128], F32)
        SSMD = sb("SSMD", [128, 8], F32)
        NOPT = sb("NOPT", [128, 2], F32)
        P.dve_nop = None
        NA_ = 17000
        AR = sb("AR", [128, NA_], F32)

        def arf(off, n):
            assert off + n <= NA_ - 3584, (off, n)
            return AR[:, off:off + n]

        def arb(off, nb):
            assert nb % 2 == 0 and off + nb // 2 <= NA_ - 3584, (off, nb)
            return AR[:, off:off + nb // 2].bitcast(BF16)

        t0 = NA_ - 3584
        HIDb = AR[:, t0:t0 + 1024].bitcast(BF16)
        SQb = AR[:, t0 + 1024:t0 + 1536].bitcast(BF16)
        RSf = AR[:, t0 + 1536:t0 + 2560]
        TMf = AR[:, t0 + 2560:t0 + 3584]

        ps = [psum("ps%d" % i, [128, 512], F32) for i in range(7)]
        pmisc = psum("pmisc", [128, 512], F32)
        pT = pmisc[:, 0:256].bitcast(BF16)
        pada = pmisc[:, 256:352]

        WSK = [("WS", 0, 0), ("WS", 0, 1), ("WS", 1, 0), ("WS", 1, 1)]

        def mm(out, lhsT, rhs, st, sp_, r, w):
            return P.op("pe", lambda e: e.matmul(out, lhsT, rhs, start=st, stop=sp_), r, w)

        def tr(out, in_, ident, r, w):
            return P.op("pe", lambda e: e.transpose(out, in_, ident), r, w)

        def act(out, in_, func, r, w, **kw):
            return P.op("act", lambda e: e.activation(out=out, in_=in_, func=func, **kw), r, w)

        def tt(eng, out, a, b, op, r, w):
            return P.op(eng, lambda e: e.tensor_tensor(out=out, in0=a, in1=b, op=op), r, w)

        def ts(eng, out, a, s1, s2, op0, op1, r, w):
            if s2 is None:
                return P.op(eng, lambda e: e.tensor_scalar(out=out, in0=a, scalar1=s1, scalar2=None, op0=op0), r, w)
            return P.op(eng, lambda e: e.tensor_scalar(out=out, in0=a, scalar1=s1, scalar2=s2, op0=op0, op1=op1), r, w)

        def stt(out, in0, scalar, in1, op0, op1, r, w):
            return P.op("dve", lambda e: e.scalar_tensor_tensor(out=out, in0=in0, scalar=scalar, in1=in1, op0=op0, op1=op1), r, w)

        def cp(eng, out, in_, r, w):
            if eng == "act":
                return P.op("act", lambda e: e.activation(out=out, in_=in_, func=AF.Identity), r, w)
            return P.op(eng, lambda e: e.tensor_copy(out=out, in_=in_), r, w)

        def memset(eng, ap, val, w):
            return P.op(eng, lambda e: e.memset(ap, val), (), w)

        def recip(out, in_, r, w):
            return P.op("dve", lambda e: e.reciprocal(out=out, in_=in_), r, w)

        def scan(out, d0, d1, init, r, w):
            return P.op("dve", lambda e: e.tensor_tensor_scan(out=out, data0=d0, data1=d1, initial=init,
                                                              op0=ALU.mult, op1=ALU.add), r, w)

        def barrier():
            P.op("dve", lambda e: e.memset(LSM[:, 15:16], 0.0), (), ["PH"])

        def rev(ap2d):
            n = ap2d.shape[1]
            a = ap2d.ap
            return bass.AP(ap2d.tensor, ap2d.offset + (n - 1) * a[1][0], [list(a[0]), [-a[1][0], n]])

        def bc_last(ap2d, n):
            a = ap2d.ap
            return bass.AP(ap2d.tensor, ap2d.offset, [list(a[0]), list(a[1]), [0, n]])

        gstate = {"g": 0}

        def gbank(pool=(0, 1, 2, 3)):
            i = pool[gstate["g"] % len(pool)]
            gstate["g"] += 1
            return i

        def xk(t, c):
            return ("X", t, c)

        def hk(t, c):
            return ("H", t, c)

        for t in range(NT):
            P.dma(X[:, t, :], xT[t * 128:(t + 1) * 128, :], w=[xk(t, c) for c in range(5)])
        P.dma(CC[:], cc_d, w=["CC"])
        P.dma(BADA[:], b_ada_d, w=["BADA"])
        P.dma(NG[:], ng_d, w=["NG"])
        P.dma(IDB[:], ident_d, w=["IDB"], eng="pool")
        P.dma(LAMV[:], lamv_d, w=["LAMV"])
        P.dma(G08[:], subg_d, w=["G08"])
        P.dma(SSMD[:], ssmd_d, w=["SSMD"])
        memset("pool", ONESB[:], 1.0, ["ONESB"])
        act(SC[:].rearrange("p t j -> p (t j)"), CC[:], AF.Silu, ["CC"], ["SC"])
        ts("dve", G08[:], G08[:], 0.8, None, ALU.mult, None, ["G08"], ["G08"])
        tt("dve", LAMV[:, 0:64], LAMV[:, 0:64], LAMV[:, 64:128], ALU.mult, ["LAMV"], ["LAMV"])
        tt("dve", LAMV[:, 128:192], LAMV[:, 128:192], LAMV[:, 192:256], ALU.mult, ["LAMV"], ["LAMV"])
        P.op("dve", lambda e: e.reduce_sum(out=LSM[:, 0:1], in_=LAMV[:, 0:64], axis=mybir.AxisListType.X), ["LAMV"], ["LSM0"])
        P.op("dve", lambda e: e.reduce_sum(out=LSM[:, 1:2], in_=LAMV[:, 128:192], axis=mybir.AxisListType.X), ["LAMV"], ["LSM1"])
        act(LSM[:, 2:4], LSM[:, 0:2], AF.Exp, ["LSM0", "LSM1"], ["LSM2"])
        tt("dve", LSM[:, 4:5], LSM[:, 3:4], LSM[:, 2:3], ALU.subtract, ["LSM2"], ["LSM4"])
        ts("dve", LSM[:, 5:6], LSM[:, 4:5], -0.2, None, ALU.add, None, ["LSM4"], ["NEGLAM"])
        NEGLAM = LSM[:, 5:6]

        ada_state = {"n": 0}

        def ada_group(l, grp):
            for jt in range(8):
                j = grp * 8 + jt
                s = ada_state["n"] % 2
                ada_state["n"] += 1
                src = w_ada[l, :, j * 128:(j + 1) * 128].rearrange("(t p) c -> p t c", p=128)
                P.dma(WAD[:, s], src, w=[("WAD", s)])
                for k in range(NT):
                    mm(pada[:, j * 2:j * 2 + 2], WAD[:, s, k, :], SC[:, k, :], k == 0, k == NT - 1,
                       [("WAD", s), "SC"], [("pada", j)])
            for j2 in range(2):
                tt("dve", MOD[:, l, grp * 8:(grp + 1) * 8, j2],
                   pada[:, grp * 16:(grp + 1) * 16].rearrange("p (t j) -> p t j", j=2)[:, :, j2],
                   BADA[:, l * 48 + grp * 8: l * 48 + (grp + 1) * 8], ALU.add,
                   [("pada", grp * 8 + jt) for jt in range(8)] + ["BADA"], [("MOD", l, grp, j2)])

        def ada_scale(l, which):
            grp = 1 if which == 0 else 4
            goff = (0 if which == 0 else 16) + l * 8
            for j2 in range(2):
                stt(AMOD[:, l, which, :, j2], MOD[:, l, grp * 8:(grp + 1) * 8, j2], 1.0, NG[:, goff:goff + 8],
                    ALU.add, ALU.mult, [("MOD", l, grp, j2), "NG"], [("AMOD", l, which, j2)])

        ncnt = {"n": 0}

        def rms_stats(ci, rs):
            c0, n = CH[ci]
            b = gbank()
            for t in range(NT):
                s = ncnt["n"] % 2
                ncnt["n"] += 1
                act(SQb[:, s * 512:s * 512 + n], X[:, t, c0:c0 + n], AF.Square, [xk(t, ci)], [("SQ", s)])
                mm(ps[b][:, :n], ONESB[:], SQb[:, s * 512:s * 512 + n], t == 0, t == NT - 1,
                   ["ONESB", ("SQ", s)], [("PS", b)])
            act(RSf[:, rs * 512:rs * 512 + n], ps[b][:, :n], AF.Sqrt, [("PS", b)], [("RS", rs)], bias=1e-6, scale=1.0 / D)
            recip(RSf[:, rs * 512:rs * 512 + n], RSf[:, rs * 512:rs * 512 + n], [("RS", rs)], [("RS", rs)])

        def norm_mod(l, which, chunks, hcol_of, dup_ctx_col=None):
            shg = 0 if which == 0 else 3
            for ci in chunks:
                c0, n = CH[ci]
                j2 = 1 if ci == 4 else 0
                rs = ci % 2
                rms_stats(ci, rs)
                hc = hcol_of(ci)
                for t in range(NT):
                    s = ncnt["n"] % 2
                    ncnt["n"] += 1
                    tt("dve", TMf[:, s * 512:s * 512 + n], X[:, t, c0:c0 + n], RSf[:, rs * 512:rs * 512 + n], ALU.mult,
                       [xk(t, ci), ("RS", rs)], [("TM", s)])
                    rk = [("TM", s), ("AMOD", l, which, j2), ("MOD", l, shg, j2)]
                    act(H[:, t, hc:hc + n], TMf[:, s * 512:s * 512 + n], AF.Identity, rk, [hk(t, ci)],
                        scale=AMOD[:, l, which, t, j2:j2 + 1], bias=MOD[:, l, shg * 8 + t, j2:j2 + 1])
                    if dup_ctx_col is not None and ci == 4:
                        act(H[:, t, dup_ctx_col:dup_ctx_col + n], TMf[:, s * 512:s * 512 + n], AF.Identity, rk, [hk(t, 5)],
                            scale=AMOD[:, l, which, t, j2:j2 + 1], bias=MOD[:, l, shg * 8 + t, j2:j2 + 1])

        def mlp(l, chunks, hcol_of):
            hcnt = 0

            def load_piece(pc):
                s = pc % 2
                W1 = WS[:, s * 4096: s * 4096 + 2048]
                W2 = WS[:, s * 4096 + 2048: s * 4096 + 4096]
                P.dma(W1.rearrange("p (t c) -> p t c", t=NT),
                      mlp_w1[l, :, pc * HID_PIECE:(pc + 1) * HID_PIECE].rearrange("(t p) c -> p t c", p=128),
                      w=[("WS", s, 0)], eng="pool")
                P.dma(W2.rearrange("p (t c) -> p t c", t=2),
                      mlp_w2[l, pc * HID_PIECE:(pc + 1) * HID_PIECE, :].rearrange("(t p) c -> p t c", p=128),
                      w=[("WS", s, 1)], eng="pool")

            load_piece(0)
            for pc in range(NPIECE):
                s = pc % 2
                W1 = WS[:, s * 4096: s * 4096 + 2048]
                W2 = WS[:, s * 4096 + 2048: s * 4096 + 4096]
                if pc + 1 < NPIECE:
                    load_piece(pc + 1)
                for ci in chunks:
                    c0, n = CH[ci]
                    j2 = 1 if ci == 4 else 0
                    hc = hcol_of(ci)
                    hs = hcnt % 2
                    hcnt += 1
                    for ht in range(2):
                        b = gbank((0, 1, 2, 3, 4, 5, 6))
                        for k in range(NT):
                            mm(ps[b][:, :n], W1[:, k * 256 + ht * 128: k * 256 + (ht + 1) * 128], H[:, k, hc:hc + n],
                               k == 0, k == NT - 1, [("WS", s, 0), hk(k, ci)], [("PS", b)])
                        hsl = HIDb[:, hs * 1024 + ht * 512: hs * 1024 + ht * 512 + n]
                        act(hsl, ps[b][:, :n], AF.Relu, [("PS", b)], [("HID", hs, ht)])
                        act(hsl, hsl, AF.Square, [("HID", hs, ht)], [("HID", hs, ht)])
                    for dt in range(NT):
                        b = gbank((0, 1, 2, 3, 4, 5, 6))
                        for k in range(2):
                            mm(ps[b][:, :n], W2[:, k * 1024 + dt * 128: k * 1024 + (dt + 1) * 128],
                               HIDb[:, hs * 1024 + k * 512: hs * 1024 + k * 512 + n], k == 0, k == 1,
                               [("WS", s, 1), ("HID", hs, k)], [("PS", b)])
                        stt(X[:, dt, c0:c0 + n], ps[b][:, :n], MOD[:, l, 40 + dt, j2:j2 + 1], X[:, dt, c0:c0 + n],
                            ALU.mult, ALU.add, [("PS", b), ("MOD", l, 5, j2), xk(dt, ci)], [xk(dt, ci)])

        def layer0():
            l = 0
            ada_group(l, 0)
            ada_group(l, 1)
            ada_scale(l, 0)
            o = 0
            Q = arb(o, NTOK); o += NTOK // 2
            K = arb(o, NTOK); o += NTOK // 2
            OTr = arb(o, NTOK); o += NTOK // 2
            V = arb(o, NKB * 130).rearrange("p (b c) -> p b c", c=130); o += NKB * 65
            E = [arb(o + i * 256, 512) for i in range(4)]; o += 1024
            RB = arb(o, 5376); o += 2688
            OT = [arb(o + i * 64, 128) for i in range(2)]; o += 128
            ZR = arb(o, 512); o += 256
            memset("pool", ZR, 0.0, ["ZR"])
            T1 = [arf(o + i * 512, 512) for i in range(2)]; o += 1024
            T2 = [arf(o + i * 512, 512) for i in range(2)]; o += 1024
            DD = [arf(o + i * 128, 128) for i in range(2)]; o += 256
            TT_ = [arf(o + i * 128, 128) for i in range(2)]; o += 256
            SMF = arf(o, 64); o += 64
            JNK = arf(o, 128); o += 128

            norm_mod(l, 0, range(5), lambda ci: CH[ci][0])
            ada_group(l, 2)

            COS = RB[:, 0:NTOK]
            SIN = RB[:, NTOK:2 * NTOK]
            P.dma(RB[:, 0:2 * NTOK], rope_d, w=["RB"], eng="pool")
            cnts = {"e": 0, "t": 0, "acc": 0, "pt": 0, "ot": 0, "sm": 0}

            WO = WS[:, 6144:7168]

            def load_WO(fidx):
                P.dma(WO, w_out[fidx * 128:(fidx + 1) * 128, :], w=[WSK[3]], eng="pool")

            def load_WIA(h):
                P.dma(WS[:, 0:5120].rearrange("p (t c) -> p t c", t=NT), w_inA[h].rearrange("(t p) c -> p t c", p=128),
                      w=WSK[0:3], eng="pool")

            def load_WIB(hp):
                P.dma(WS[:, 0:3072].rearrange("p (t c) -> p t c", t=NT), w_inB[hp].rearrange("(t p) c -> p t c", p=128),
                      w=WSK[0:3], eng="pool")

            def out_proj(fidx):
                for ci in range(5):
                    c0, n = CH[ci]
                    j2 = 1 if ci == 4 else 0
                    for dt in range(NT):
                        b = gbank()
                        mm(ps[b][:, :n], WO[:, dt * 128:(dt + 1) * 128], OTr[:, c0:c0 + n], True, True,
                           [WSK[3], ("OTr", ci)], [("PS", b)])
                        stt(X[:, dt, c0:c0 + n], ps[b][:, :n], MOD[:, l, 16 + dt, j2:j2 + 1], X[:, dt, c0:c0 + n],
                            ALU.mult, ALU.add, [("PS", b), ("MOD", l, 2, j2), xk(dt, ci)], [xk(dt, ci)])

            def transpose_out(ot_slot, tb):
                pslot = cnts["pt"] % 4
                cnts["pt"] += 1
                tr(pT[:, pslot * 128:(pslot + 1) * 128], OT[ot_slot], IDB[:],
                   [("OT", ot_slot, 0), ("OT", ot_slot, 1), "IDB"], [("pT", pslot)])
                cp("act", OTr[:, tb * 128:(tb + 1) * 128], pT[:, pslot * 128:(pslot + 1) * 128],
                   [("pT", pslot)], [("OTr", tb // 4)])

            def v_proj(WI, c_lo, evac):
                for g4 in range(5):
                    nb = 4 if g4 < 4 else 2
                    b = gbank()
                    for i in range(nb):
                        tb = g4 * 4 + i
                        for k in range(NT):
                            mm(ps[b][:, i * 128:(i + 1) * 128], H[:, k, tb * 128:(tb + 1) * 128], WI[:, k, c_lo:c_lo + 128],
                               k == 0, k == NT - 1, WSK[0:3] + [hk(k, tb // 4)], [("PS", b)])
                    evac(g4, nb, b)

            load_WIA(0)
            for h in range(4):
                WI = WS[:, 0:5120].rearrange("p (t c) -> p t c", t=NT)
                load_WO(h)
                for (dst, dname, c_q, c_s) in ((Q, "Q", 0, 128), (K, "K", 256, 384)):
                    for ci in range(5):
                        c0, n = CH[ci]
                        b1 = gbank()
                        b2 = gbank()
                        for k in range(NT):
                            mm(ps[b1][:, :n], WI[:, k, c_q:c_q + 128], H[:, k, c0:c0 + n], k == 0, k == NT - 1,
                               WSK[0:3] + [hk(k, ci)], [("PS", b1)])
                        for k in range(NT):
                            mm(ps[b2][:, :n], WI[:, k, c_s:c_s + 128], H[:, k, c0:c0 + n], k == 0, k == NT - 1,
                               WSK[0:3] + [hk(k, ci)], [("PS", b2)])
                        s = cnts["t"] % 2
                        cnts["t"] += 1
                        tt("dve", T1[s][:, :n], ps[b1][:, :n], COS[:, c0:c0 + n], ALU.mult, [("PS", b1), "RB"], [("T1", s)])
                        tt("dve", T2[s][:, :n], ps[b2][:, :n], SIN[:, c0:c0 + n], ALU.mult, [("PS", b2), "RB"], [("T2", s)])
                        tt("pool", dst[:, c0:c0 + n], T1[s][:, :n], T2[s][:, :n], ALU.add, [("T1", s), ("T2", s)], [(dname, ci)])
                if h == 0:
                    memset("pool", V[:, :, 128:129], 1.0, [("V", kb) for kb in range(NKB)])

                def evacA(g4, nb, b):
                    cp("act", V[:, g4 * 4:g4 * 4 + nb, 0:128], ps[b][:, :nb * 128].rearrange("p (b c) -> p b c", c=128),
                       [("PS", b)], [("V", g4 * 4 + i) for i in range(nb)])
                v_proj(WI, 512, evacA)
                if h < 3:
                    load_WIA(h + 1)
                else:
                    load_WIB(0)

                for qc in range(5):
                    c0, n = CH[qc]
                    nqb = n // 128
                    kbs = list(range(NKB)) if qc < 4 else [16, 17]
                    accs = {}
                    for qb in range(nqb):
                        for m in range(2):
                            a = qb * 2 + m
                            accs[(qb, m)] = (4 + a // 3, (a % 3) * 129)
                    for bk in sorted(set(v[0] for v in accs.values())):
                        offs = sorted(v[1] for v in accs.values() if v[0] == bk)
                        wid = offs[-1] + 129
                        mm(ps[bk][:, 0:wid], ZR[:, 0:128], ZR[:, 0:wid], True, True, ["ZR"], [("ACC", bk, of_) for of_ in offs])
                    for m in range(2):
                        for ki, kb in enumerate(kbs):
                            b = gbank()
                            mm(ps[b][:, :n], K[m * 64:(m + 1) * 64, kb * 128:(kb + 1) * 128], Q[m * 64:(m + 1) * 64, c0:c0 + n],
                               True, True, [("K", kb // 4), ("Q", qc)], [("PS", b)])
                            es_ = cnts["e"] % 4
                            cnts["e"] += 1
                            act(E[es_][:, :n], ps[b][:, :n], AF.Exp, [("PS", b)], [("E", es_)], scale=0.125)
                            for qb in range(nqb):
                                bk, off = accs[(qb, m)]
                                mm(ps[bk][:, off:off + 129], E[es_][:, qb * 128:(qb + 1) * 128], V[:, kb, 0:129],
                                   False, ki == len(kbs) - 1, [("E", es_), ("V", kb)], [("ACC", bk, off)])
                    for qb in range(nqb):
                        b0, o0 = accs[(qb, 0)]
                        b1, o1 = accs[(qb, 1)]
                        k0 = ("ACC", b0, o0)
                        k1 = ("ACC", b1, o1)
                        sm = (cnts["sm"] % 2) * 8
                        cnts["sm"] += 1
                        ds = cnts["ot"] % 2
                        cnts["ot"] += 1
                        recip(SMF[:, sm:sm + 1], ps[b0][:, o0 + 128:o0 + 129], [k0], [("SMF", sm, 0)])
                        recip(SMF[:, sm + 1:sm + 2], ps[b1][:, o1 + 128:o1 + 129], [k1], [("SMF", sm, 1)])
                        tt("dve", SMF[:, sm + 2:sm + 3], SMF[:, sm + 1:sm + 2], NEGLAM, ALU.mult,
                           [("SMF", sm, 1), "NEGLAM"], [("SMF", sm, 2)])
                        ts("dve", TT_[ds], ps[b1][:, o1:o1 + 128], SMF[:, sm + 2:sm + 3], None, ALU.mult, None,
                           [k1, ("SMF", sm, 2)], [("TT", ds)])
                        stt(DD[ds], ps[b0][:, o0:o0 + 128], SMF[:, sm:sm + 1], TT_[ds], ALU.mult, ALU.add,
                            [k0, ("SMF", sm, 0), ("TT", ds)], [("DD", ds)])
                        act(JNK, DD[ds], AF.Square, [("DD", ds)], ["JNK", ("SMF", sm, 3)], accum_out=SMF[:, sm + 3:sm + 4])
                        act(SMF[:, sm + 4:sm + 5], SMF[:, sm + 3:sm + 4], AF.Sqrt, [("SMF", sm, 3)], [("SMF", sm, 4)],
                            bias=1e-5, scale=1.0 / 128)
                        recip(SMF[:, sm + 5:sm + 6], SMF[:, sm + 4:sm + 5], [("SMF", sm, 4)], [("SMF", sm, 5)])
                        stt(OT[ds], DD[ds], SMF[:, sm + 5:sm + 6], G08[:], ALU.mult, ALU.mult,
                            [("DD", ds), ("SMF", sm, 5), "G08"], [("OT", ds, 0), ("OT", ds, 1)])
                        transpose_out(ds, c0 // 128 + qb)
                out_proj(h)

            Vb = V.rearrange("p b (h c) -> p b h c", c=65)
            for hp in range(4):
                WI = WS[:, 0:3072].rearrange("p (t c) -> p t c", t=NT)
                load_WO(4 + hp)
                P.dma(RB, bias_d[hp], w=["RB"], eng="pool")
                for ci in range(5):
                    c0, n = CH[ci]
                    b1 = gbank()
                    b2 = gbank()
                    for k in range(NT):
                        mm(ps[b1][:, :n], WI[:, k, 0:128], H[:, k, c0:c0 + n], k == 0, k == NT - 1,
                           WSK[0:3] + [hk(k, ci)], [("PS", b1)])
                    for k in range(NT):
                        mm(ps[b2][:, :n], WI[:, k, 128:256], H[:, k, c0:c0 + n], k == 0, k == NT - 1,
                           WSK[0:3] + [hk(k, ci)], [("PS", b2)])
                    act(Q[:, c0:c0 + n], ps[b1][:, :n], AF.Identity, [("PS", b1)], [("Q", ci)], scale=0.125)
                    cp("dve", K[:, c0:c0 + n], ps[b2][:, :n], [("PS", b2)], [("K", ci)])

                def evacB(g4, nb, b):
                    cp("act", Vb[:, g4 * 4:g4 * 4 + nb, :, 0:64],
                       ps[b][:, :nb * 128].rearrange("p (b h c) -> p b h c", h=2, c=64),
                       [("PS", b)], [("V", g4 * 4 + i) for i in range(nb)])
                v_proj(WI, 256, evacB)
                if hp < 3:
                    load_WIB(hp + 1)
                if hp == 0:
                    memset("pool", Vb[:, :, :, 64:65], 1.0, [("V", kb) for kb in range(NKB)])
                for j in range(NKB):
                    kbs = (na_kbs(j) + [16, 17]) if j < 16 else [16, 17]
                    batches = [kbs[i:i + 4] for i in range(0, len(kbs), 4)]
                    ds = cnts["ot"] % 2
                    cnts["ot"] += 1
                    for hh in range(2):
                        pr = slice(hh * 64, (hh + 1) * 64)
                        a = cnts["acc"] % 21
                        cnts["acc"] += 1
                        bk, off = 4 + a // 7, (a % 7) * 65
                        kacc = ("ACC", bk, off)
                        nk = 0
                        for bt in batches:
                            b = gbank()
                            for i, kb in enumerate(bt):
                                lat = kb < 16
                                mm(ps[b][:, i * 128:(i + 1) * 128], K[pr, kb * 128:(kb + 1) * 128], Q[pr, j * 128:(j + 1) * 128],
                                   True, not lat, [("K", kb // 4), ("Q", j // 4)], [("PS", b)])
                                if lat:
                                    ti = hh * NBT + na_tile(j, kb)
                                    mm(ps[b][:, i * 128:(i + 1) * 128], IDB[:], RB[:, ti * 128:(ti + 1) * 128],
                                       False, True, ["IDB", "RB"], [("PS", b)])
                            es_ = cnts["e"] % 4
                            cnts["e"] += 1
                            w_ = len(bt) * 128
                            act(E[es_][:, :w_], ps[b][:, :w_], AF.Exp, [("PS", b)], [("E", es_)])
                            for i, kb in enumerate(bt):
                                mm(ps[bk][:, off:off + 65], E[es_][:, i * 128:(i + 1) * 128], Vb[:, kb, hh, :],
                                   nk == 0, nk == len(kbs) - 1, [("E", es_), ("V", kb)], [kacc])
                                nk += 1
                        sm = (cnts["sm"] % 2) * 8
                        cnts["sm"] += 1
                        recip(SMF[:, sm:sm + 1], ps[bk][:, off + 64:off + 65], [kacc], [("SMF", sm, 0)])
                        ts("dve", OT[ds][:, hh * 64:(hh + 1) * 64], ps[bk][:, off:off + 64], SMF[:, sm:sm + 1], None,
                           ALU.mult, None, [kacc, ("SMF", sm, 0)], [("OT", ds, hh)])
                    transpose_out(ds, j)
                out_proj(4 + hp)

            ada_group(l, 3)
            ada_group(l, 4)
            ada_scale(l, 1)
            ada_group(l, 5)
            norm_mod(l, 1, range(5), lambda ci: CH[ci][0])
            mlp(l, range(5), lambda ci: CH[ci][0])

        def layer1():
            l = 1
            MAGIC = 12582912.0
            hcol = lambda ci: (256 + CH[ci][0]) if ci < 4 else 0
            barrier()
            ada_group(l, 0)
            ada_group(l, 1)
            ada_scale(l, 0)
            norm_mod(l, 0, range(5), hcol, dup_ctx_col=2304)
            ada_group(l, 2)
            ada_group(l, 3)
            ada_group(l, 4)
            ada_scale(l, 1)
            ada_group(l, 5)
            if _STOP == "a":
                return

            o = 0
            PR = [arf(o + d * 96, 96) for d in range(2)]; o += 192
            RHO = [arf(o + d * 32, 32) for d in range(2)]; o += 64
            TH = [arf(o + d * 32, 32) for d in range(2)]; o += 64
            tmp_off = o; o += 32 * 24
            CTAB = [arb(o + d * 512, 1024) for d in range(2)]; o += 1024
            BB = [arb(o + d * 512, 1024) for d in range(2)]; o += 1024
            IOTA = arf(o, 512); o += 512
            TAB = [[arf(o + s * 1024, 512), arf(o + s * 1024 + 512, 512)] for s in range(2)]; o += 2048
            o_ys = o
            YS = [arf(o + i * 512, 512) for i in range(3)]; o += 1536
            PP = [arf(o + i * 512, 512) for i in range(2)]; o += 1024
            o_mm = o
            MM = [arf(o + i * 512, 512) for i in range(4)]; o += 2048
            CRI = [arf(o + i * 512, 512) for i in range(2)]; o += 1024
            VRI = [arf(o + i * 512, 512) for i in range(2)]; o += 1024
            ZZ = PP
            CAR = arf(o, 16); o += 16
            BRAW = [arf(o_mm + d * 1024, 1024) for d in range(2)]
            CRAW = [arf(o_ys + d * 1024, 1024) for d in range(2)]
            BLs = [WS[:, 0:2048], WS[:, 2048:4096]]
            CLs = [WS[:, 4096:6144], WS[:, 6144:8192]]
            WADb = WAD[:].rearrange("p a t c -> p (a t c)").bitcast(BF16)
            SRI = [WADb[:, i * 512:(i + 1) * 512] for i in range(4)]
            TIN = WADb[:, 2048:4096]

            P.dma(IOTA, iota_d, w=["IOTA"])
            for d in range(2):
                P.dma(PR[d], s5p_d[d], w=[("PR", d)])
                P.dma(BRAW[d], s5b_d[d], w=[("BRAW", d)])
                P.dma(CRAW[d], s5c_d[d], w=[("CRAW", d)])

            for d in range(2):
                tcount = {"n": 0}

                def tmp():
                    i = tcount["n"]
                    tcount["n"] += 1
                    assert i < 24
                    return arf(tmp_off + i * 32, 32), ("S5T", i)

                LR, LI, LS = PR[d][:, 0:32], PR[d][:, 32:64], PR[d][:, 64:96]
                pk = ("PR", d)
                DT, kDT = tmp()
                act(DT, LS, AF.Exp, [pk], [kDT])
                LRD, kLRD = tmp()
                tt("dve", LRD, LR, DT, ALU.mult, [pk, kDT], [kLRD])
                act(RHO[d], LRD, AF.Exp, [kLRD], [("RHO", d)])
                ANG, kANG = tmp()
                tt("dve", ANG, LI, DT, ALU.mult, [pk, kDT], [kANG])
                Y, kY = tmp()
                ts("dve", Y, ANG, 1.0 / TWO_PI, None, ALU.mult, None, [kANG], [kY])
                KY, kKY = tmp()
                ts("dve", KY, Y, MAGIC, MAGIC, ALU.add, ALU.subtract, [kY], [kKY])
                tt("dve", TH[d], Y, KY, ALU.subtract, [kY, kKY], [("TH", d)])
                SINA, kS = tmp()
                act(SINA, TH[d], AF.Sin, [("TH", d)], [kS], scale=TWO_PI)
                Y2, kY2 = tmp()
                ts("dve", Y2, TH[d], 0.25, None, ALU.add, None, [("TH", d)], [kY2])
                KY2, kKY2 = tmp()
                ts("dve", KY2, Y2, MAGIC, MAGIC, ALU.add, ALU.subtract, [kY2], [kKY2])
                FR2, kF2 = tmp()
                tt("dve", FR2, Y2, KY2, ALU.subtract, [kY2, kKY2], [kF2])
                COSA, kC = tmp()
                act(COSA, FR2, AF.Sin, [kF2], [kC], scale=TWO_PI)
                AR_, kAR = tmp()
                tt("dve", AR_, RHO[d], COSA, ALU.mult, [("RHO", d), kC], [kAR])
                AI_, kAI = tmp()
                tt("dve", AI_, RHO[d], SINA, ALU.mult, [("RHO", d), kS], [kAI])
                NR, kNR = tmp()
                ts("dve", NR, AR_, -1.0, None, ALU.add, None, [kAR], [kNR])
                T_, kT = tmp()
                tt("dve", T_, LR, LR, ALU.mult, [pk], [kT])
                DEN, kDEN = tmp()
                tt("dve", DEN, LI, LI, ALU.mult, [pk], [kDEN])
                tt("dve", DEN, DEN, T_, ALU.add, [kDEN, kT], [kDEN])
                recip(DEN, DEN, [kDEN], [kDEN])
                C1, k1 = tmp()
                tt("dve", C1, NR, LR, ALU.mult, [kNR, pk], [k1])
                C2, k2 = tmp()
                tt("dve", C2, AI_, LI, ALU.mult, [kAI, pk], [k2])
                tt("dve", C1, C1, C2, ALU.add, [k1, k2], [k1])
                CRv, kCR = tmp()
                tt("dve", CRv, C1, DEN, ALU.mult, [k1, kDEN], [kCR])
                C3, k3 = tmp()
                tt("dve", C3, AI_, LR, ALU.mult, [kAI, pk], [k3])
                C4, k4 = tmp()
                tt("dve", C4, NR, LI, ALU.mult, [kNR, pk], [k4])
                tt("dve", C3, C3, C4, ALU.subtract, [k3, k4], [k3])
                CIv, kCI = tmp()
                tt("dve", CIv, C3, DEN, ALU.mult, [k3, kDEN], [kCI])
                BR3 = BRAW[d][:, 0:512].rearrange("p (s h) -> p s h", h=16)
                BI3 = BRAW[d][:, 512:1024].rearrange("p (s h) -> p s h", h=16)
                U = [x.rearrange("p (s h) -> p s h", h=16) for x in (VRI[0], VRI[1], CRI[0], CRI[1])]
                crb = bc_last(CRv, 16)
                cib = bc_last(CIv, 16)
                bk_ = ("BRAW", d)
                tt("dve", U[0], BR3, crb, ALU.mult, [bk_, kCR], ["U0"])
                tt("dve", U[1], BI3, cib, ALU.mult, [bk_, kCI], ["U1"])
                tt("dve", BB[d][:, 0:512].rearrange("p (s h) -> p s h", h=16), U[0], U[1], ALU.subtract, ["U0", "U1"], [("BB", d)])
                tt("dve", U[2], BI3, crb, ALU.mult, [bk_, kCR], ["U2"])
                tt("dve", U[3], BR3, cib, ALU.mult, [bk_, kCI], ["U3"])
                tt("dve", BB[d][:, 512:1024].rearrange("p (s h) -> p s h", h=16), U[2], U[3], ALU.add, ["U2", "U3"], [("BB", d)])
                cp("dve", CTAB[d][:, 0:512], CRAW[d][:, 0:512], [("CRAW", d)], [("CTAB", d)])
                ts("dve", CTAB[d][:, 512:1024], CRAW[d][:, 512:1024], -1.0, None, ALU.mult, None, [("CRAW", d)], [("CTAB", d)])
            barrier()
            memset("pool", TIN, 0.0, [("TIN", 0), ("TIN", 1)])
            memset("pool", CLs[0], 0.0, [("CL", 0, i) for i in range(8)])
            memset("pool", CLs[1], 0.0, [("CL", 1, i) for i in range(8)])
            if _STOP == "b":
                return

            cn = {"tab": 0, "sr": 0, "z": 0, "pt": 0, "car": 0}
            def blk(ap2d, base, n4, step4):
                a = ap2d.ap
                return bass.AP(ap2d.tensor, ap2d.offset + base * a[1][0], [list(a[0]), [step4 * a[1][0], n4], [a[1][0], 16]])

            def build_tables(ct, bs):
                BLc, CLc = BLs[bs], CLs[bs]
                for dr in range(2):
                    for ri in range(2):
                        for gl in range(2):
                            rows = slice(gl * 64, (gl + 1) * 64)
                            src_b = blk(BB[dr][rows, :], ri * 512 + 4 * ct * 16, 4, 16)
                            src_c = blk(CTAB[dr][rows, :], ri * 512 + 4 * ct * 16, 4, 16)
                            dst_t = blk(TIN[rows, :], dr * 1024 + ri * 128 + gl * 16, 4, 288)
                            dst_c = blk(CLc[rows, :], dr * 1024 + ri * 128 + gl * 16, 4, 288)
                            cp("dve", dst_t, src_b, [("BB", dr)], [("TIN", dr)])
                            cp("dve", dst_c, src_c, [("CTAB", dr)], [("CL", bs, dr * 4 + s_) for s_ in range(4)])
                    for g4 in range(2):
                        for q in range(4):
                            ti = dr * 8 + g4 * 4 + q
                            tr(pT[:, q * 128:(q + 1) * 128], TIN[:, ti * 128:(ti + 1) * 128], IDB[:],
                               [("TIN", dr), "IDB"], [("pT", q)])
                        c0 = (dr * 8 + g4 * 4) * 128
                        cp("act", BLc[:, c0:c0 + 512], pT[:, 0:512], [("pT", q) for q in range(4)],
                           [("BL", bs, dr * 4 + g4 * 2), ("BL", bs, dr * 4 + g4 * 2 + 1)])

            cts = range(int(_STOP[1:])) if (_STOP and _STOP[0] == "e" and _STOP[1:].isdigit()) else range(NT)
            if _STOP and _STOP[0] == "x":
                cts = tuple(int(ch) for ch in _STOP[1:])
            for ci_, ct in enumerate(cts):
                bs = ci_ % 2
                build_tables(ct, bs)
                P.dma(s5tab[ct, :, 0:2048], BLs[bs], r=[("BL", bs, i) for i in range(8)], w=[("SCR", ct, 0)])
                P.dma(s5tab[ct, :, 2048:4096], CLs[bs], r=[("CL", bs, i) for i in range(8)], w=[("SCR", ct, 1)])
            barrier()

            if _STOP == "pa":
                return

            def load_tables(ct, bs):
                P.dma(BLs[bs], s5tab[ct, :, 0:2048], r=[("SCR", ct, 0)], w=[("BL", bs, i) for i in range(8)])
                P.dma(CLs[bs], s5tab[ct, :, 2048:4096], r=[("SCR", ct, 1)], w=[("CL", bs, i) for i in range(8)])

            load_tables(cts[0], 0)
            for ci_, ct in enumerate(cts):
                bs = ci_ % 2
                BLc, CLc = BLs[bs], CLs[bs]
                if ci_ + 1 < len(cts):
                    load_tables(cts[ci_ + 1], 1 - bs)
                for dr in range(2):
                    if dr == 0:
                        seq = [(0, 256, None, 4)] + [(256 + c * 512, 512, c, c) for c in range(4)]
                    else:
                        seq = [(2304, 256, None, 5)] + [(256 + c * 512, 512, c, c) for c in (3, 2, 1, 0)]
                    for stl in range(4):
                        st = 4 * ct + stl
                        idx = dr * 4 + stl
                        slot = cn["tab"] % 2
                        cn["tab"] += 1
                        COSt, SINt = TAB[slot]
                        kt = ("TAB", slot)
                        ts("pool", YS[0], IOTA, TH[dr][:, st:st + 1], None, ALU.mult, None, ["IOTA", ("TH", dr)], ["YS0"])
                        ts("dve", YS[1], YS[0], MAGIC, MAGIC, ALU.add, ALU.subtract, ["YS0"], ["YS1"])
                        tt("pool", YS[0], YS[0], YS[1], ALU.subtract, ["YS0", "YS1"], ["YS0"])
                        act(SINt, YS[0], AF.Sin, ["YS0"], [(kt, 1)], scale=TWO_PI)
                        ts("pool", YS[2], YS[0], 0.25, None, ALU.add, None, ["YS0"], ["YS2"])
                        ts("dve", YS[1], YS[2], MAGIC, MAGIC, ALU.add, ALU.subtract, ["YS2"], ["YS1"])
                        tt("pool", YS[2], YS[2], YS[1], ALU.subtract, ["YS2", "YS1"], ["YS2"])
                        act(COSt, YS[2], AF.Sin, ["YS2"], [(kt, 0)], scale=TWO_PI)
                        tabk = [(kt, 0), (kt, 1)]
                        if _STOP == "d1":
                            return
                        rho = RHO[dr][:, st:st + 1]
                        for i, (c0, n, yc, hkc) in enumerate(seq):
                            b_r = gbank((0, 1, 2))
                            b_i = gbank((0, 1, 2))
                            mm(ps[b_r][:, :n], BLc[:, (idx * 2) * 128:(idx * 2 + 1) * 128], H[:, ct, c0:c0 + n], True, True,
                               [("BL", bs, idx), hk(ct, hkc)], [("PS", b_r)])
                            mm(ps[b_i][:, :n], BLc[:, (idx * 2 + 1) * 128:(idx * 2 + 2) * 128], H[:, ct, c0:c0 + n], True, True,
                               [("BL", bs, idx), hk(ct, hkc)], [("PS", b_i)])
                            if dr == 0:
                                cosv, sinv = COSt[:, :n], SINt[:, :n]
                                fw = lambda a: a
                            else:
                                cosv, sinv = rev(COSt[:, :n]), rev(SINt[:, :n])
                                fw = rev
                            tt("dve", MM[0][:, :n], ps[b_r][:, :n], cosv, ALU.mult, [("PS", b_r)] + tabk, ["MM0"])
                            tt("dve", MM[1][:, :n], ps[b_i][:, :n], sinv, ALU.mult, [("PS", b_i)] + tabk, ["MM1"])
                            tt("pool", CRI[0][:, :n], MM[0][:, :n], MM[1][:, :n], ALU.add, ["MM0", "MM1"], ["CRI0"])
                            tt("dve", MM[2][:, :n], ps[b_i][:, :n], cosv, ALU.mult, [("PS", b_i)] + tabk, ["MM2"])
                            tt("dve", MM[3][:, :n], ps[b_r][:, :n], sinv, ALU.mult, [("PS", b_r)] + tabk, ["MM3"])
                            tt("pool", CRI[1][:, :n], MM[2][:, :n], MM[3][:, :n], ALU.subtract, ["MM2", "MM3"], ["CRI1"])
                            if i == 0:
                                ini_r, ini_i, ik = 0.0, 0.0, []
                            else:
                                cs = (cn["car"] % 2) * 2
                                ini_r, ini_i, ik = CAR[:, cs:cs + 1], CAR[:, cs + 1:cs + 2], [("CAR", cs)]
                            scan(fw(VRI[0][:, :n]), rho.to_broadcast([128, n]), fw(CRI[0][:, :n]), ini_r,
                                 ["CRI0", ("RHO", dr)] + ik, ["VRI0"])
                            scan(fw(VRI[1][:, :n]), rho.to_broadcast([128, n]), fw(CRI[1][:, :n]), ini_i,
                                 ["CRI1", ("RHO", dr)] + ik, ["VRI1"])
                            if i < len(seq) - 1:
                                last = n - 1 if dr == 0 else 0
                                cn["car"] += 1
                                cs = (cn["car"] % 2) * 2
                                cN, sN = COSt[:, n - 1:n], SINt[:, n - 1:n]
                                vrl, vil = VRI[0][:, last:last + 1], VRI[1][:, last:last + 1]
                                tt("dve", CAR[:, 8:9], vil, sN, ALU.mult, ["VRI1"] + tabk, ["CAR8"])
                                stt(CAR[:, cs:cs + 1], vrl, cN, CAR[:, 8:9], ALU.mult, ALU.subtract, ["VRI0", "CAR8"] + tabk, [("CAR", cs)])
                                tt("dve", CAR[:, 9:10], vil, cN, ALU.mult, ["VRI1"] + tabk, ["CAR9"])
                                stt(CAR[:, cs + 1:cs + 2], vrl, sN, CAR[:, 9:10], ALU.mult, ALU.add, ["VRI0", "CAR9"] + tabk, [("CAR", cs)])
                            if yc is not None:
                                srs = (cn["sr"] % 2) * 2
                                cn["sr"] += 1
                                tt("pool", PP[0][:, :n], VRI[0][:, :n], cosv, ALU.mult, ["VRI0"] + tabk, ["PP0"])
                                tt("pool", PP[1][:, :n], VRI[1][:, :n], sinv, ALU.mult, ["VRI1"] + tabk, ["PP1"])
                                tt("pool", SRI[srs][:, :n], PP[0][:, :n], PP[1][:, :n], ALU.subtract, ["PP0", "PP1"], [("SRI", srs)])
                                tt("pool", PP[0][:, :n], VRI[0][:, :n], sinv, ALU.mult, ["VRI0"] + tabk, ["PP0"])
                                tt("pool", PP[1][:, :n], VRI[1][:, :n], cosv, ALU.mult, ["VRI1"] + tabk, ["PP1"])
                                tt("pool", SRI[srs + 1][:, :n], PP[0][:, :n], PP[1][:, :n], ALU.add, ["PP0", "PP1"], [("SRI", srs + 1)])
                                first = (dr == 0 and stl == 0)
                                lastm = (dr == 1 and stl == 3)
                                mm(ps[3 + yc][:, :n], CLc[:, (idx * 2) * 128:(idx * 2 + 1) * 128], SRI[srs][:, :n], first, False,
                                   [("CL", bs, idx), ("SRI", srs)], [("PS", 3 + yc)])
                                mm(ps[3 + yc][:, :n], CLc[:, (idx * 2 + 1) * 128:(idx * 2 + 2) * 128], SRI[srs + 1][:, :n], False, lastm,
                                   [("CL", bs, idx), ("SRI", srs + 1)], [("PS", 3 + yc)])
                        if _STOP == "d":
                            return
                    if _STOP == "d2":
                        return
                for yc in range(4):
                    c0h = 256 + yc * 512
                    z = cn["z"] % 2
                    cn["z"] += 1
                    stt(ZZ[z], H[:, ct, c0h:c0h + 512], SSMD[:, ct:ct + 1], ps[3 + yc][:, :], ALU.mult, ALU.add,
                        [hk(ct, yc), "SSMD", ("PS", 3 + yc)], ["PP%d" % z])
                    G1_, G2_ = CRI[z], VRI[z]
                    k1_, k2_ = "CRI%d" % z, "VRI%d" % z
                    tt("pool", G1_, ZZ[z], ZZ[z], ALU.mult, ["PP%d" % z], [k1_])
                    ts("pool", G1_, G1_, 0.044715, 1.0, ALU.mult, ALU.add, [k1_], [k1_])
                    tt("pool", G1_, G1_, ZZ[z], ALU.mult, [k1_, "PP%d" % z], [k1_])
                    act(G2_, G1_, AF.Sigmoid, [k1_], [k2_], scale=1.5957691216)
                    tt("pool", H[:, ct, c0h:c0h + 512], ZZ[z], G2_, ALU.mult, ["PP%d" % z, k2_], [hk(ct, yc)])

            barrier()
            if _STOP and _STOP[0] in "ex":
                return

            def load_glu(dt):
                s = dt % 2
                P.dma(WS[:, s * 4096: s * 4096 + 1024].rearrange("p (t c) -> p t c", t=NT),
                      glu_a[:, dt * 128:(dt + 1) * 128].rearrange("(t p) c -> p t c", p=128), w=[("WS", s, 0)], eng="pool")
                P.dma(WS[:, s * 4096 + 1024: s * 4096 + 2048].rearrange("p (t c) -> p t c", t=NT),
                      glu_b[:, dt * 128:(dt + 1) * 128].rearrange("(t p) c -> p t c", p=128), w=[("WS", s, 0)], eng="pool")

            load_glu(0)
            for dt in range(NT):
                s = dt % 2
                GA = WS[:, s * 4096: s * 4096 + 1024]
                GB = WS[:, s * 4096 + 1024: s * 4096 + 2048]
                if dt + 1 < NT:
                    load_glu(dt + 1)
                for yc in range(4):
                    c0, n = CH[yc]
                    hc = 256 + c0
                    bA = gbank((0, 1, 2, 3, 4, 5, 6))
                    bB = gbank((0, 1, 2, 3, 4, 5, 6))
                    for k in range(NT):
                        mm(ps[bA][:, :n], GA[:, k * 128:(k + 1) * 128], H[:, k, hc:hc + n], k == 0, k == NT - 1,
                           [("WS", s, 0), hk(k, yc)], [("PS", bA)])
                    for k in range(NT):
                        mm(ps[bB][:, :n], GB[:, k * 128:(k + 1) * 128], H[:, k, hc:hc + n], k == 0, k == NT - 1,
                           [("WS", s, 0), hk(k, yc)], [("PS", bB)])
                    z = cn["z"] % 2
                    cn["z"] += 1
                    act(ZZ[z], ps[bB][:, :n], AF.Sigmoid, [("PS", bB)], [("ZZ", z)])
                    tt("dve", VRI[z], ps[bA][:, :n], ZZ[z], ALU.mult, [("PS", bA), ("ZZ", z)], [("GT", z)])
                    stt(X[:, dt, c0:c0 + n], VRI[z], MOD[:, l, 16 + dt, 0:1], X[:, dt, c0:c0 + n], ALU.mult, ALU.add,
                        [("GT", z), ("MOD", l, 2, 0), xk(dt, yc)], [xk(dt, yc)])
            if _STOP == "f":
                return

            norm_mod(l, 1, range(4), hcol)
            mlp(l, range(4), hcol)

        out_ops = []
        if n_layers >= 1:
            layer0()
        if n_layers >= 2:
            layer1()
        if dbg:
            for t in range(NT):
                out_ops.append(P.dma(outT[t * 128:(t + 1) * 128, :], X[:, t, :], r=[xk(t, c) for c in range(5)]))
        else:
            OS = [AR[:, i * 512:(i + 1) * 512] for i in range(4)]
            barrier()
            ocnt = 0
            for ci in range(4):
                c0, n = CH[ci]
                rs = ci % 2
                rms_stats(ci, rs)
                for t in range(NT):
                    s = ncnt["n"] % 2
                    ncnt["n"] += 1
                    tt("dve", TMf[:, s * 512:s * 512 + n], X[:, t, c0:c0 + n], RSf[:, rs * 512:rs * 512 + n], ALU.mult,
                       [xk(t, ci), ("RS", rs)], [("TM", s)])
                    osl = ocnt % 4
                    ocnt += 1
                    act(OS[osl], TMf[:, s * 512:s * 512 + n], AF.Identity, [("TM", s), "NG"], [("OS", osl)],
                        scale=NG[:, 32 + t:33 + t])
                    out_ops.append(P.dma(outT[t * 128:(t + 1) * 128, c0:c0 + n], OS[osl], r=[("OS", osl)]))
        P.emit(final_wait_ops=out_ops)
    return nc


def _rope_tables():
    t = np.arange(NLAT)
    row = (t // 64).astype(np.float32)
    col = (t % 64).astype(np.float32)
    inv_freq = (10000.0 ** (-np.arange(16, dtype=np.float32) / 16)).astype(np.float32)
    ang_r = row[:, None] * inv_freq[None, :]
    ang_c = col[:, None] * inv_freq[None, :]
    cos = np.ones((128, NTOK), np.float32)
    sin = np.zeros((128, NTOK), np.float32)
    for p in range(128):
        d = p % 64
        f = d % 16
        ang = ang_r[:, f] if d < 32 else ang_c[:, f]
        sgn = -1.0 if (d % 32) < 16 else 1.0
        cos[p, :NLAT] = np.cos(ang)
        sin[p, :NLAT] = sgn * np.sin(ang)
    return np.concatenate([cos, sin], axis=1).astype(np.float32)


def _swap_idx():
    idx = np.arange(128)
    d = idx % 64
    partner = np.where((d % 32) < 16, d + 16, d - 16)
    return (idx // 64) * 64 + partner


def _bias_tables(rpb):
    out = np.full((8, NBT, 128, 128), -30000.0, np.float32)
    kl = np.arange(128)[:, None]
    ql = np.arange(128)[None, :]
    combos = [(5, 5 + dlt, dlt + 2) for dlt in range(-2, 3)]
    combos += [(j, kb, na_tile(j, kb)) for j in (0, 1, 14, 15) for kb in na_kbs(j)]
    for (j, kb, ti) in combos:
        qrow = 2 * j + ql // 64
        qcol = ql % 64
        krow = 2 * kb + kl // 64
        kcol = kl % 64
        rs = np.clip(qrow - 4, 0, 24)
        cs = np.clip(qcol - 8, 0, 48)
        valid = (krow >= rs) & (krow < rs + 8) & (kcol >= cs) & (kcol < cs + 16)
        ro = np.clip(krow - qrow + 7, 0, 14)
        co = np.clip(kcol - qcol + 15, 0, 30)
        for h in range(8):
            out[h, ti] = np.where(valid, rpb[h][ro, co], np.float32(-30000.0))
    return out


def _prep(inp):
    f = np.float32
    x = np.asarray(inp["x"], f)
    ctx = np.asarray(inp["ctx"], f)
    c = np.asarray(inp["c"], f)
    c_ctx = np.asarray(inp["c_ctx"], f)
    shared = {}
    shared["w_ada"] = np.ascontiguousarray(inp["w_ada"], f)
    shared["b_ada"] = np.ascontiguousarray(np.asarray(inp["b_ada"], f).reshape(2, 48, 128).transpose(2, 0, 1).reshape(128, 96))
    ng = np.concatenate([np.asarray(inp["norm1_g"], f).reshape(2, 8, 128), np.asarray(inp["norm2_g"], f).reshape(2, 8, 128),
                         np.asarray(inp["final_g"], f).reshape(1, 8, 128)], axis=0)
    shared["ng"] = np.ascontiguousarray(ng.transpose(2, 0, 1).reshape(128, 40))
    w_in = np.asarray(inp["w_in"], f)[0]
    sw = _swap_idx()
    wA = np.empty((4, D, 640), f)
    wB = np.empty((4, D, 384), f)
    for h in range(4):
        q = w_in[:, h * 128:(h + 1) * 128]
        k = w_in[:, 512 + h * 128: 512 + (h + 1) * 128]
        v = w_in[:, 1024 + h * 128: 1024 + (h + 1) * 128]
        wA[h] = np.concatenate([q, q[:, sw], k, k[:, sw], v], axis=1)
        wB[h] = np.concatenate([w_in[:, 1536 + h * 128:1536 + (h + 1) * 128], w_in[:, 2048 + h * 128:2048 + (h + 1) * 128],
                                w_in[:, 2560 + h * 128:2560 + (h + 1) * 128]], axis=1)
    shared["w_inA"] = wA
    shared["w_inB"] = wB
    shared["w_out"] = np.ascontiguousarray(np.asarray(inp["w_out"], f)[0])
    lamv = np.concatenate([np.asarray(inp[k], f)[0] for k in ("lam_q1", "lam_k1", "lam_q2", "lam_k2")])
    shared["lamv"] = np.ascontiguousarray(np.broadcast_to(lamv[None, :], (128, 256)))
    shared["subg"] = np.ascontiguousarray(np.broadcast_to(np.asarray(inp["subln_g"], f)[0][None, :], (128, 128)))
    bt = _bias_tables(np.asarray(inp["na_rpb"], f)[0])
    bt = bt.reshape(4, 2, NBT, 128, 128).transpose(0, 3, 1, 2, 4).reshape(4, 128, 2 * NBT * 128)
    shared["biasT"] = np.ascontiguousarray(bt)
    shared["rope"] = _rope_tables()
    shared["ident"] = np.eye(128, dtype=f)
    shared["iota1"] = np.ascontiguousarray(np.broadcast_to(np.arange(1, 513, dtype=f)[None, :], (128, 512)))

    def st_layout(a):
        sh = a.shape
        a = a.reshape((2, 32, 2, 64) + sh[3:])
        perm = (0, 2, 3, 1) + tuple(range(4, a.ndim))
        a = a.transpose(perm)
        return a.reshape((2, 128, 32) + sh[3:])
    lre = st_layout(np.asarray(inp["ssm_lam_re"], f)[0])
    lim = st_layout(np.asarray(inp["ssm_lam_im"], f)[0])
    lst = st_layout(np.ascontiguousarray(np.broadcast_to(np.asarray(inp["ssm_log_step"], f)[0][:, :, None], (2, 64, 64))))
    shared["s5p"] = np.ascontiguousarray(np.concatenate([lre, lim, lst], axis=2))
    bre = st_layout(np.asarray(inp["ssm_b_re"], f)[0]).reshape(2, 128, 512)
    bim = st_layout(np.asarray(inp["ssm_b_im"], f)[0]).reshape(2, 128, 512)
    shared["s5b"] = np.ascontiguousarray(np.concatenate([bre, bim], axis=2))
    cre = st_layout(np.asarray(inp["ssm_c_re"], f)[0].transpose(0, 1, 3, 2)).reshape(2, 128, 512)
    cim = st_layout(np.asarray(inp["ssm_c_im"], f)[0].transpose(0, 1, 3, 2)).reshape(2, 128, 512)
    shared["s5c"] = np.ascontiguousarray(np.concatenate([cre, cim], axis=2))
    shared["ssmd"] = np.ascontiguousarray(np.asarray(inp["ssm_d"], f)[0].reshape(8, 128).T)
    shared["glu_w_a"] = np.ascontiguousarray(np.asarray(inp["glu_w_a"], f)[0])
    shared["glu_w_b"] = np.ascontiguousarray(np.asarray(inp["glu_w_b"], f)[0])
    shared["mlp_w1"] = np.ascontiguousarray(inp["mlp_w1"], f)
    shared["mlp_w2"] = np.ascontiguousarray(inp["mlp_w2"], f)
    maps = []
    for b in range(8):
        m = dict(shared)
        m["xT"] = np.ascontiguousarray(np.concatenate([x[b].T, ctx[b].T], axis=1))
        cc = np.stack([c[b].reshape(8, 128).T, c_ctx.reshape(8, 128).T], axis=2)
        m["cc"] = np.ascontiguousarray(cc.reshape(128, 16))
        maps.append(m)
    return maps


_NC_CACHE = {}


def kernel(**inputs):
    maps = _prep(inputs)
    if "nc" not in _NC_CACHE:
        _NC_CACHE["nc"] = build()
    res = run_bass_kernel_spmd(_NC_CACHE["nc"], maps, core_ids=list(range(8)))
    out = np.stack([np.ascontiguousarray(r["outT"].T) for r in res.results], axis=0)
    return out.astype(np.float32)
```

```python
import math
import numpy as np
from contextlib import ExitStack
import concourse.bass as bass
import concourse.mybir as mybir
from concourse.bass_utils import run_bass_kernel_spmd

F32 = mybir.dt.float32
BF16 = mybir.dt.bfloat16
I32 = mybir.dt.int32
ALU = mybir.AluOpType
AF = mybir.ActivationFunctionType

ENGS = ("pe", "act", "dve", "pool", "sp")
NDMASEM = 24
TWO_PI = 2.0 * math.pi


class Op:
    __slots__ = ("eng", "fn", "deps", "signal", "sigval", "idx", "is_dma", "sem", "target")

    def __init__(self, eng, fn, is_dma=False):
        self.eng = eng
        self.fn = fn
        self.deps = []
        self.signal = False
        self.sigval = 0
        self.idx = -1
        self.is_dma = is_dma
        self.sem = None
        self.target = 0


class Prog:
    def __init__(self, nc):
        self.nc = nc
        self.ops = {e: [] for e in ENGS}
        self.last_w = {}
        self.readers = {}
        self.waited = {e: {} for e in ENGS}
        self.waited_dma = {e: {} for e in ENGS}
        self.dma_count = 0
        self.dma_last = [None] * NDMASEM
        self.dma_tot = [0] * NDMASEM
        self.n_usem = 0
        self.dve_nop = None

    def op(self, eng, fn, r=(), w=(), is_dma=False):
        o = Op(eng, fn, is_dma)
        r = list(r) + ["PH"]
        deps = []
        for k in r:
            lw = self.last_w.get(k)
            if lw is not None:
                deps.append(lw)
        for k in w:
            lw = self.last_w.get(k)
            if lw is not None:
                deps.append(lw)
            deps.extend(self.readers.get(k, ()))
        if is_dma and eng == "pool" and UNIQUE_POOL_SEMS:
            o.sem = NDMASEM + self.n_usem
            self.n_usem += 1
            o.target = 16
        elif is_dma:
            k = self.dma_count % NDMASEM
            self.dma_count += 1
            prev = self.dma_last[k]
            if prev is not None:
                deps.append(prev)
            self.dma_tot[k] += 16
            o.sem = k
            o.target = self.dma_tot[k]
            self.dma_last[k] = o
        best = {}
        need_nop = False
        for d in deps:
            if d is o:
                continue
            if d.is_dma:
                if self.waited_dma[eng].get(d.sem, 0) >= d.target:
                    continue
                key = ("d", d.sem)
                if key not in best or best[key].target < d.target:
                    best[key] = d
            else:
                if d.eng == "pe" and eng == "pe":
                    continue
                if d.eng == "dve" and eng == "dve" and self.dve_nop is not None:
                    if d.idx == len(self.ops[eng]) - 1:
                        need_nop = True
                    continue
                if self.waited[eng].get(d.eng, -1) >= d.idx:
                    continue
                key = ("c", d.eng)
                if key not in best or best[key].idx < d.idx:
                    best[key] = d
        for key, d in best.items():
            if d.is_dma:
                self.waited_dma[eng][d.sem] = d.target
            else:
                self.waited[eng][d.eng] = d.idx
                d.signal = True
            o.deps.append(d)
        if need_nop:
            nop = Op(eng, self.dve_nop, False)
            nop.idx = len(self.ops[eng])
            self.ops[eng].append(nop)
        o.idx = len(self.ops[eng])
        self.ops[eng].append(o)
        for k in r:
            self.readers.setdefault(k, []).append(o)
        for k in w:
            self.last_w[k] = o
            self.readers[k] = []
        return o

    def dma(self, out, in_, r=(), w=(), eng="sp"):
        return self.op(eng, lambda e: e.dma_start(out=out, in_=in_), r=r, w=w, is_dma=True)

    def emit(self, final_wait_ops=()):
        nc = self.nc
        with ExitStack() as es:
            esem = {e: es.enter_context(nc.semaphore("s_" + e)) for e in ENGS}
            dsem = [es.enter_context(nc.semaphore("d_%d" % i)) for i in range(NDMASEM + self.n_usem)]
            total = {}
            for e in ENGS:
                comp = [o for o in self.ops[e] if not o.is_dma]
                if comp:
                    comp[-1].signal = True
                c = 0
                for o in self.ops[e]:
                    if o.is_dma:
                        continue
                    if o.signal:
                        c += 1
                    o.sigval = c
                total[e] = c
            block = es.enter_context(nc.Block())
            engmap = {"pe": block.tensor, "act": block.scalar, "dve": block.vector,
                      "pool": block.gpsimd, "sp": block.sync}

            def make(e):
                def body(eng):
                    for o in self.ops[e]:
                        for d in o.deps:
                            if d.is_dma:
                                eng.wait_ge(dsem[d.sem], d.target)
                            else:
                                eng.wait_ge(esem[d.eng], d.sigval)
                        ins = o.fn(eng)
                        if o.is_dma:
                            ins.then_inc(dsem[o.sem], 16)
                        elif o.signal:
                            ins.then_inc(esem[e], 1)
                    for f in ENGS:
                        if f != e and total[f] > 0:
                            eng.wait_ge(esem[f], total[f])
                    if e == "sp":
                        for k in range(NDMASEM):
                            if self.dma_tot[k] > 0:
                                eng.wait_ge(dsem[k], self.dma_tot[k])
                return body

            for e in ENGS:
                if self.ops[e] or e == "sp":
                    engmap[e](make(e))


D = 1024
NT = 8
NLAT = 2048
NCTX = 256
NTOK = NLAT + NCTX
CH = [(0, 512), (512, 512), (1024, 512), (1536, 512), (2048, 256)]
NKB = 18
HID_PIECE = 256
NPIECE = 4096 // HID_PIECE
NBT = 21
_STOP = None
UNIQUE_POOL_SEMS = False


def na_kbs(j):
    if j < 2:
        return [0, 1, 2, 3]
    if j > 13:
        return [12, 13, 14, 15]
    return [j - 2, j - 1, j, j + 1, j + 2]


def na_tile(j, kb):
    if j < 2:
        return 5 + j * 4 + kb
    if j > 13:
        return 13 + (j - 14) * 4 + (kb - 12)
    return kb - j + 2


def build(n_layers=2, dbg=False):
    nc = bass.Bass("TRN2", target_bir_lowering=False)
    P = Prog(nc)

    def din(name, shape, dt=F32):
        return nc.dram_tensor(name, list(shape), dt, kind="ExternalInput").ap()

    xT = din("xT", [D, NTOK])
    cc_d = din("cc", [128, 16])
    w_ada = din("w_ada", [2, D, 6 * D])
    b_ada_d = din("b_ada", [128, 96])
    ng_d = din("ng", [128, 40])
    w_inA = din("w_inA", [4, D, 640])
    w_inB = din("w_inB", [4, D, 384])
    w_out = din("w_out", [D, D])
    lamv_d = din("lamv", [128, 256])
    subg_d = din("subg", [128, 128])
    bias_d = din("biasT", [4, 128, 2 * NBT * 128])
    rope_d = din("rope", [128, 2 * NTOK])
    ident_d = din("ident", [128, 128])
    iota_d = din("iota1", [128, 512])
    s5p_d = din("s5p", [2, 128, 96])
    s5b_d = din("s5b", [2, 128, 1024])
    s5c_d = din("s5c", [2, 128, 1024])
    ssmd_d = din("ssmd", [128, 8])
    glu_a = din("glu_w_a", [D, D])
    glu_b = din("glu_w_b", [D, D])
    mlp_w1 = din("mlp_w1", [2, D, 4 * D])
    mlp_w2 = din("mlp_w2", [2, 4 * D, D])
    s5tab = nc.dram_tensor("s5tab", [NT, 128, 4096], BF16).ap()
    n_out_tok = NTOK if dbg else NLAT
    outT = nc.dram_tensor("outT", [D, n_out_tok], F32, kind="ExternalOutput").ap()

    with ExitStack() as es:
        def sb(name, shape, dt):
            return es.enter_context(nc.sbuf_tensor(name, list(shape), dt))

        def psum(name, shape, dt):
            return es.enter_context(nc.psum_tensor(name, list(shape), dt))

        X = sb("X", [128, NT, NTOK], F32)
        H = sb("H", [128, NT, 2560], BF16)
        WS = sb("WS", [128, 8192], BF16)
        WAD = sb("WAD", [128, 2, NT, 128], F32)
        CC = sb("CC", [128, 16], F32)
        SC = sb("SC", [128, NT, 2], F32)
        MOD = sb("MOD", [128, 2, 48, 2], F32)
        BADA = sb("BADA", [128, 96], F32)
        NG = sb("NG", [128, 40], F32)
        AMOD = sb("AMOD", [128, 2, 2, NT, 2], F32)
        ONESB = sb("ONESB", [128, 128], BF16)
        IDB = sb("IDB", [128, 128], BF16)
        LAMV = sb("LAMV", [128, 256], F32)
        LSM = sb("LSM", [128, 16], F32)
        G08 = sb("G08", [128, 128], F32)
        SSMD = sb("SSMD", [128, 8], F32)
        NOPT = sb("NOPT", [128, 2], F32)
        P.dve_nop = None
        NA_ = 17000
        AR = sb("AR", [128, NA_], F32)

        def arf(off, n):
            assert off + n <= NA_ - 3584, (off, n)
            return AR[:, off:off + n]

        def arb(off, nb):
            assert nb % 2 == 0 and off + nb // 2 <= NA_ - 3584, (off, nb)
            return AR[:, off:off + nb // 2].bitcast(BF16)

        t0 = NA_ - 3584
        HIDb = AR[:, t0:t0 + 1024].bitcast(BF16)
        SQb = AR[:, t0 + 1024:t0 + 1536].bitcast(BF16)
        RSf = AR[:, t0 + 1536:t0 + 2560]
        TMf = AR[:, t0 + 2560:t0 + 3584]

        ps = [psum("ps%d" % i, [128, 512], F32) for i in range(7)]
        pmisc = psum("pmisc", [128, 512], F32)
        pT = pmisc[:, 0:256].bitcast(BF16)
        pada = pmisc[:, 256:352]

        WSK = [("WS", 0, 0), ("WS", 0, 1), ("WS", 1, 0), ("WS", 1, 1)]

        def mm(out, lhsT, rhs, st, sp_, r, w):
            return P.op("pe", lambda e: e.matmul(out, lhsT, rhs, start=st, stop=sp_), r, w)

        def tr(out, in_, ident, r, w):
            return P.op("pe", lambda e: e.transpose(out, in_, ident), r, w)

        def act(out, in_, func, r, w, **kw):
            return P.op("act", lambda e: e.activation(out=out, in_=in_, func=func, **kw), r, w)

        def tt(eng, out, a, b, op, r, w):
            return P.op(eng, lambda e: e.tensor_tensor(out=out, in0=a, in1=b, op=op), r, w)

        def ts(eng, out, a, s1, s2, op0, op1, r, w):
            if s2 is None:
                return P.op(eng, lambda e: e.tensor_scalar(out=out, in0=a, scalar1=s1, scalar2=None, op0=op0), r, w)
            return P.op(eng, lambda e: e.tensor_scalar(out=out, in0=a, scalar1=s1, scalar2=s2, op0=op0, op1=op1), r, w)

        def stt(out, in0, scalar, in1, op0, op1, r, w):
            return P.op("dve", lambda e: e.scalar_tensor_tensor(out=out, in0=in0, scalar=scalar, in1=in1, op0=op0, op1=op1), r, w)

        def cp(eng, out, in_, r, w):
            if eng == "act":
                return P.op("act", lambda e: e.activation(out=out, in_=in_, func=AF.Identity), r, w)
            return P.op(eng, lambda e: e.tensor_copy(out=out, in_=in_), r, w)

        def memset(eng, ap, val, w):
            return P.op(eng, lambda e: e.memset(ap, val), (), w)

        def recip(out, in_, r, w):
            return P.op("dve", lambda e: e.reciprocal(out=out, in_=in_), r, w)

        def scan(out, d0, d1, init, r, w):
            return P.op("dve", lambda e: e.tensor_tensor_scan(out=out, data0=d0, data1=d1, initial=init,
                                                              op0=ALU.mult, op1=ALU.add), r, w)

        def barrier():
            P.op("dve", lambda e: e.memset(LSM[:, 15:16], 0.0), (), ["PH"])

        def rev(ap2d):
            n = ap2d.shape[1]
            a = ap2d.ap
            return bass.AP(ap2d.tensor, ap2d.offset + (n - 1) * a[1][0], [list(a[0]), [-a[1][0], n]])

        def bc_last(ap2d, n):
            a = ap2d.ap
            return bass.AP(ap2d.tensor, ap2d.offset, [list(a[0]), list(a[1]), [0, n]])

        gstate = {"g": 0}

        def gbank(pool=(0, 1, 2, 3)):
            i = pool[gstate["g"] % len(pool)]
            gstate["g"] += 1
            return i

        def xk(t, c):
            return ("X", t, c)

        def hk(t, c):
            return ("H", t, c)

        for t in range(NT):
            P.dma(X[:, t, :], xT[t * 128:(t + 1) * 128, :], w=[xk(t, c) for c in range(5)])
        P.dma(CC[:], cc_d, w=["CC"])
        P.dma(BADA[:], b_ada_d, w=["BADA"])
        P.dma(NG[:], ng_d, w=["NG"])
        P.dma(IDB[:], ident_d, w=["IDB"], eng="pool")
        P.dma(LAMV[:], lamv_d, w=["LAMV"])
        P.dma(G08[:], subg_d, w=["G08"])
        P.dma(SSMD[:], ssmd_d, w=["SSMD"])
        memset("pool", ONESB[:], 1.0, ["ONESB"])
        act(SC[:].rearrange("p t j -> p (t j)"), CC[:], AF.Silu, ["CC"], ["SC"])
        ts("dve", G08[:], G08[:], 0.8, None, ALU.mult, None, ["G08"], ["G08"])
        tt("dve", LAMV[:, 0:64], LAMV[:, 0:64], LAMV[:, 64:128], ALU.mult, ["LAMV"], ["LAMV"])
        tt("dve", LAMV[:, 128:192], LAMV[:, 128:192], LAMV[:, 192:256], ALU.mult, ["LAMV"], ["LAMV"])
        P.op("dve", lambda e: e.reduce_sum(out=LSM[:, 0:1], in_=LAMV[:, 0:64], axis=mybir.AxisListType.X), ["LAMV"], ["LSM0"])
        P.op("dve", lambda e: e.reduce_sum(out=LSM[:, 1:2], in_=LAMV[:, 128:192], axis=mybir.AxisListType.X), ["LAMV"], ["LSM1"])
        act(LSM[:, 2:4], LSM[:, 0:2], AF.Exp, ["LSM0", "LSM1"], ["LSM2"])
        tt("dve", LSM[:, 4:5], LSM[:, 3:4], LSM[:, 2:3], ALU.subtract, ["LSM2"], ["LSM4"])
        ts("dve", LSM[:, 5:6], LSM[:, 4:5], -0.2, None, ALU.add, None, ["LSM4"], ["NEGLAM"])
        NEGLAM = LSM[:, 5:6]

        ada_state = {"n": 0}

        def ada_group(l, grp):
            for jt in range(8):
                j = grp * 8 + jt
                s = ada_state["n"] % 2
                ada_state["n"] += 1
                src = w_ada[l, :, j * 128:(j + 1) * 128].rearrange("(t p) c -> p t c", p=128)
                P.dma(WAD[:, s], src, w=[("WAD", s)])
                for k in range(NT):
                    mm(pada[:, j * 2:j * 2 + 2], WAD[:, s, k, :], SC[:, k, :], k == 0, k == NT - 1,
                       [("WAD", s), "SC"], [("pada", j)])
            for j2 in range(2):
                tt("dve", MOD[:, l, grp * 8:(grp + 1) * 8, j2],
                   pada[:, grp * 16:(grp + 1) * 16].rearrange("p (t j) -> p t j", j=2)[:, :, j2],
                   BADA[:, l * 48 + grp * 8: l * 48 + (grp + 1) * 8], ALU.add,
                   [("pada", grp * 8 + jt) for jt in range(8)] + ["BADA"], [("MOD", l, grp, j2)])

        def ada_scale(l, which):
            grp = 1 if which == 0 else 4
            goff = (0 if which == 0 else 16) + l * 8
            for j2 in range(2):
                stt(AMOD[:, l, which, :, j2], MOD[:, l, grp * 8:(grp + 1) * 8, j2], 1.0, NG[:, goff:goff + 8],
                    ALU.add, ALU.mult, [("MOD", l, grp, j2), "NG"], [("AMOD", l, which, j2)])

        ncnt = {"n": 0}

        def rms_stats(ci, rs):
            c0, n = CH[ci]
            b = gbank()
            for t in range(NT):
                s = ncnt["n"] % 2
                ncnt["n"] += 1
                act(SQb[:, s * 512:s * 512 + n], X[:, t, c0:c0 + n], AF.Square, [xk(t, ci)], [("SQ", s)])
                mm(ps[b][:, :n], ONESB[:], SQb[:, s * 512:s * 512 + n], t == 0, t == NT - 1,
                   ["ONESB", ("SQ", s)], [("PS", b)])
            act(RSf[:, rs * 512:rs * 512 + n], ps[b][:, :n], AF.Sqrt, [("PS", b)], [("RS", rs)], bias=1e-6, scale=1.0 / D)
            recip(RSf[:, rs * 512:rs * 512 + n], RSf[:, rs * 512:rs * 512 + n], [("RS", rs)], [("RS", rs)])

        def norm_mod(l, which, chunks, hcol_of, dup_ctx_col=None):
            shg = 0 if which == 0 else 3
            for ci in chunks:
                c0, n = CH[ci]
                j2 = 1 if ci == 4 else 0
                rs = ci % 2
                rms_stats(ci, rs)
                hc = hcol_of(ci)
                for t in range(NT):
                    s = ncnt["n"] % 2
                    ncnt["n"] += 1
                    tt("dve", TMf[:, s * 512:s * 512 + n], X[:, t, c0:c0 + n], RSf[:, rs * 512:rs * 512 + n], ALU.mult,
                       [xk(t, ci), ("RS", rs)], [("TM", s)])
                    rk = [("TM", s), ("AMOD", l, which, j2), ("MOD", l, shg, j2)]
                    act(H[:, t, hc:hc + n], TMf[:, s * 512:s * 512 + n], AF.Identity, rk, [hk(t, ci)],
                        scale=AMOD[:, l, which, t, j2:j2 + 1], bias=MOD[:, l, shg * 8 + t, j2:j2 + 1])
                    if dup_ctx_col is not None and ci == 4:
                        act(H[:, t, dup_ctx_col:dup_ctx_col + n], TMf[:, s * 512:s * 512 + n], AF.Identity, rk, [hk(t, 5)],
                            scale=AMOD[:, l, which, t, j2:j2 + 1], bias=MOD[:, l, shg * 8 + t, j2:j2 + 1])

        def mlp(l, chunks, hcol_of):
            hcnt = 0

            def load_piece(pc):
                s = pc % 2
                W1 = WS[:, s * 4096: s * 4096 + 2048]
                W2 = WS[:, s * 4096 + 2048: s * 4096 + 4096]
                P.dma(W1.rearrange("p (t c) -> p t c", t=NT),
                      mlp_w1[l, :, pc * HID_PIECE:(pc + 1) * HID_PIECE].rearrange("(t p) c -> p t c", p=128),
                      w=[("WS", s, 0)], eng="pool")
                P.dma(W2.rearrange("p (t c) -> p t c", t=2),
                      mlp_w2[l, pc * HID_PIECE:(pc + 1) * HID_PIECE, :].rearrange("(t p) c -> p t c", p=128),
                      w=[("WS", s, 1)], eng="pool")

            load_piece(0)
            for pc in range(NPIECE):
                s = pc % 2
                W1 = WS[:, s * 4096: s * 4096 + 2048]
                W2 = WS[:, s * 4096 + 2048: s * 4096 + 4096]
                if pc + 1 < NPIECE:
                    load_piece(pc + 1)
                for ci in chunks:
                    c0, n = CH[ci]
                    j2 = 1 if ci == 4 else 0
                    hc = hcol_of(ci)
                    hs = hcnt % 2
                    hcnt += 1
                    for ht in range(2):
                        b = gbank((0, 1, 2, 3, 4, 5, 6))
                        for k in range(NT):
                            mm(ps[b][:, :n], W1[:, k * 256 + ht * 128: k * 256 + (ht + 1) * 128], H[:, k, hc:hc + n],
                               k == 0, k == NT - 1, [("WS", s, 0), hk(k, ci)], [("PS", b)])
                        hsl = HIDb[:, hs * 1024 + ht * 512: hs * 1024 + ht * 512 + n]
                        act(hsl, ps[b][:, :n], AF.Relu, [("PS", b)], [("HID", hs, ht)])
                        act(hsl, hsl, AF.Square, [("HID", hs, ht)], [("HID", hs, ht)])
                    for dt in range(NT):
                        b = gbank((0, 1, 2, 3, 4, 5, 6))
                        for k in range(2):
                            mm(ps[b][:, :n], W2[:, k * 1024 + dt * 128: k * 1024 + (dt + 1) * 128],
                               HIDb[:, hs * 1024 + k * 512: hs * 1024 + k * 512 + n], k == 0, k == 1,
                               [("WS", s, 1), ("HID", hs, k)], [("PS", b)])
                        stt(X[:, dt, c0:c0 + n], ps[b][:, :n], MOD[:, l, 40 + dt, j2:j2 + 1], X[:, dt, c0:c0 + n],
                            ALU.mult, ALU.add, [("PS", b), ("MOD", l, 5, j2), xk(dt, ci)], [xk(dt, ci)])

        def layer0():
            l = 0
            ada_group(l, 0)
            ada_group(l, 1)
            ada_scale(l, 0)
            o = 0
            Q = arb(o, NTOK); o += NTOK // 2
            K = arb(o, NTOK); o += NTOK // 2
            OTr = arb(o, NTOK); o += NTOK // 2
            V = arb(o, NKB * 130).rearrange("p (b c) -> p b c", c=130); o += NKB * 65
            E = [arb(o + i * 256, 512) for i in range(4)]; o += 1024
            RB = arb(o, 5376); o += 2688
            OT = [arb(o + i * 64, 128) for i in range(2)]; o += 128
            ZR = arb(o, 512); o += 256
            memset("pool", ZR, 0.0, ["ZR"])
            T1 = [arf(o + i * 512, 512) for i in range(2)]; o += 1024
            T2 = [arf(o + i * 512, 512) for i in range(2)]; o += 1024
            DD = [arf(o + i * 128, 128) for i in range(2)]; o += 256
            TT_ = [arf(o + i * 128, 128) for i in range(2)]; o += 256
            SMF = arf(o, 64); o += 64
            JNK = arf(o, 128); o += 128

            norm_mod(l, 0, range(5), lambda ci: CH[ci][0])
            ada_group(l, 2)

            COS = RB[:, 0:NTOK]
            SIN = RB[:, NTOK:2 * NTOK]
            P.dma(RB[:, 0:2 * NTOK], rope_d, w=["RB"], eng="pool")
            cnts = {"e": 0, "t": 0, "acc": 0, "pt": 0, "ot": 0, "sm": 0}

            WO = WS[:, 6144:7168]

            def load_WO(fidx):
                P.dma(WO, w_out[fidx * 128:(fidx + 1) * 128, :], w=[WSK[3]], eng="pool")

            def load_WIA(h):
                P.dma(WS[:, 0:5120].rearrange("p (t c) -> p t c", t=NT), w_inA[h].rearrange("(t p) c -> p t c", p=128),
                      w=WSK[0:3], eng="pool")

            def load_WIB(hp):
                P.dma(WS[:, 0:3072].rearrange("p (t c) -> p t c", t=NT), w_inB[hp].rearrange("(t p) c -> p t c", p=128),
                      w=WSK[0:3], eng="pool")

            def out_proj(fidx):
                for ci in range(5):
                    c0, n = CH[ci]
                    j2 = 1 if ci == 4 else 0
                    for dt in range(NT):
                        b = gbank()
                        mm(ps[b][:, :n], WO[:, dt * 128:(dt + 1) * 128], OTr[:, c0:c0 + n], True, True,
                           [WSK[3], ("OTr", ci)], [("PS", b)])
                        stt(X[:, dt, c0:c0 + n], ps[b][:, :n], MOD[:, l, 16 + dt, j2:j2 + 1], X[:, dt, c0:c0 + n],
                            ALU.mult, ALU.add, [("PS", b), ("MOD", l, 2, j2), xk(dt, ci)], [xk(dt, ci)])

            def transpose_out(ot_slot, tb):
                pslot = cnts["pt"] % 4
                cnts["pt"] += 1
                tr(pT[:, pslot * 128:(pslot + 1) * 128], OT[ot_slot], IDB[:],
                   [("OT", ot_slot, 0), ("OT", ot_slot, 1), "IDB"], [("pT", pslot)])
                cp("act", OTr[:, tb * 128:(tb + 1) * 128], pT[:, pslot * 128:(pslot + 1) * 128],
                   [("pT", pslot)], [("OTr", tb // 4)])

            def v_proj(WI, c_lo, evac):
                for g4 in range(5):
                    nb = 4 if g4 < 4 else 2
                    b = gbank()
                    for i in range(nb):
                        tb = g4 * 4 + i
                        for k in range(NT):
                            mm(ps[b][:, i * 128:(i + 1) * 128], H[:, k, tb * 128:(tb + 1) * 128], WI[:, k, c_lo:c_lo + 128],
                               k == 0, k == NT - 1, WSK[0:3] + [hk(k, tb // 4)], [("PS", b)])
                    evac(g4, nb, b)

            load_WIA(0)
            for h in range(4):
                WI = WS[:, 0:5120].rearrange("p (t c) -> p t c", t=NT)
                load_WO(h)
                for (dst, dname, c_q, c_s) in ((Q, "Q", 0, 128), (K, "K", 256, 384)):
                    for ci in range(5):
                        c0, n = CH[ci]
                        b1 = gbank()
                        b2 = gbank()
                        for k in range(NT):
                            mm(ps[b1][:, :n], WI[:, k, c_q:c_q + 128], H[:, k, c0:c0 + n], k == 0, k == NT - 1,
                               WSK[0:3] + [hk(k, ci)], [("PS", b1)])
                        for k in range(NT):
                            mm(ps[b2][:, :n], WI[:, k, c_s:c_s + 128], H[:, k, c0:c0 + n], k == 0, k == NT - 1,
                               WSK[0:3] + [hk(k, ci)], [("PS", b2)])
                        s = cnts["t"] % 2
                        cnts["t"] += 1
                        tt("dve", T1[s][:, :n], ps[b1][:, :n], COS[:, c0:c0 + n], ALU.mult, [("PS", b1), "RB"], [("T1", s)])
                        tt("dve", T2[s][:, :n], ps[b2][:, :n], SIN[:, c0:c0 + n], ALU.mult, [("PS", b2), "RB"], [("T2", s)])
                        tt("pool", dst[:, c0:c0 + n], T1[s][:, :n], T2[s][:, :n], ALU.add, [("T1", s), ("T2", s)], [(dname, ci)])
                if h == 0:
                    memset("pool", V[:, :, 128:129], 1.0, [("V", kb) for kb in range(NKB)])

                def evacA(g4, nb, b):
                    cp("act", V[:, g4 * 4:g4 * 4 + nb, 0:128], ps[b][:, :nb * 128].rearrange("p (b c) -> p b c", c=128),
                       [("PS", b)], [("V", g4 * 4 + i) for i in range(nb)])
                v_proj(WI, 512, evacA)
                if h < 3:
                    load_WIA(h + 1)
                else:
                    load_WIB(0)

                for qc in range(5):
                    c0, n = CH[qc]
                    nqb = n // 128
                    kbs = list(range(NKB)) if qc < 4 else [16, 17]
                    accs = {}
                    for qb in range(nqb):
                        for m in range(2):
                            a = qb * 2 + m
                            accs[(qb, m)] = (4 + a // 3, (a % 3) * 129)
                    for bk in sorted(set(v[0] for v in accs.values())):
                        offs = sorted(v[1] for v in accs.values() if v[0] == bk)
                        wid = offs[-1] + 129
                        mm(ps[bk][:, 0:wid], ZR[:, 0:128], ZR[:, 0:wid], True, True, ["ZR"], [("ACC", bk, of_) for of_ in offs])
                    def score_step(m, ki, kb):
                        b = gbank()
                        mm(ps[b][:, :n], K[m * 64:(m + 1) * 64, kb * 128:(kb + 1) * 128], Q[m * 64:(m + 1) * 64, c0:c0 + n],
                           True, True, [("K", kb // 4), ("Q", qc)], [("PS", b)])
                        es_ = cnts["e"] % 4
                        cnts["e"] += 1
                        act(E[es_][:, :n], ps[b][:, :n], AF.Exp, [("PS", b)], [("E", es_)], scale=0.125)
                        return (m, ki, kb, es_)

                    def pv_step(st_):
                        m, ki, kb, es_ = st_
                        for qb in range(nqb):
                            bk, off = accs[(qb, m)]
                            mm(ps[bk][:, off:off + 129], E[es_][:, qb * 128:(qb + 1) * 128], V[:, kb, 0:129],
                               False, ki == len(kbs) - 1, [("E", es_), ("V", kb)], [("ACC", bk, off)])

                    pend = None
                    for m in range(2):
                        for ki, kb in enumerate(kbs):
                            cur = score_step(m, ki, kb)
                            if pend is not None:
                                pv_step(pend)
                            pend = cur
                    pv_step(pend)
                    for qb in range(nqb):
                        b0, o0 = accs[(qb, 0)]
                        b1, o1 = accs[(qb, 1)]
                        k0 = ("ACC", b0, o0)
                        k1 = ("ACC", b1, o1)
                        sm = (cnts["sm"] % 2) * 8
                        cnts["sm"] += 1
                        ds = cnts["ot"] % 2
                        cnts["ot"] += 1
                        recip(SMF[:, sm:sm + 1], ps[b0][:, o0 + 128:o0 + 129], [k0], [("SMF", sm, 0)])
                        recip(SMF[:, sm + 1:sm + 2], ps[b1][:, o1 + 128:o1 + 129], [k1], [("SMF", sm, 1)])
                        tt("dve", SMF[:, sm + 2:sm + 3], SMF[:, sm + 1:sm + 2], NEGLAM, ALU.mult,
                           [("SMF", sm, 1), "NEGLAM"], [("SMF", sm, 2)])
                        ts("dve", TT_[ds], ps[b1][:, o1:o1 + 128], SMF[:, sm + 2:sm + 3], None, ALU.mult, None,
                           [k1, ("SMF", sm, 2)], [("TT", ds)])
                        stt(DD[ds], ps[b0][:, o0:o0 + 128], SMF[:, sm:sm + 1], TT_[ds], ALU.mult, ALU.add,
                            [k0, ("SMF", sm, 0), ("TT", ds)], [("DD", ds)])
                        act(JNK, DD[ds], AF.Square, [("DD", ds)], ["JNK", ("SMF", sm, 3)], accum_out=SMF[:, sm + 3:sm + 4])
                        act(SMF[:, sm + 4:sm + 5], SMF[:, sm + 3:sm + 4], AF.Sqrt, [("SMF", sm, 3)], [("SMF", sm, 4)],
                            bias=1e-5, scale=1.0 / 128)
                        recip(SMF[:, sm + 5:sm + 6], SMF[:, sm + 4:sm + 5], [("SMF", sm, 4)], [("SMF", sm, 5)])
                        stt(OT[ds], DD[ds], SMF[:, sm + 5:sm + 6], G08[:], ALU.mult, ALU.mult,
                            [("DD", ds), ("SMF", sm, 5), "G08"], [("OT", ds, 0), ("OT", ds, 1)])
                        transpose_out(ds, c0 // 128 + qb)
                out_proj(h)

            Vb = V.rearrange("p b (h c) -> p b h c", c=65)
            for hp in range(4):
                WI = WS[:, 0:3072].rearrange("p (t c) -> p t c", t=NT)
                load_WO(4 + hp)
                P.dma(RB, bias_d[hp], w=["RB"], eng="pool")
                for ci in range(5):
                    c0, n = CH[ci]
                    b1 = gbank()
                    b2 = gbank()
                    for k in range(NT):
                        mm(ps[b1][:, :n], WI[:, k, 0:128], H[:, k, c0:c0 + n], k == 0, k == NT - 1,
                           WSK[0:3] + [hk(k, ci)], [("PS", b1)])
                    for k in range(NT):
                        mm(ps[b2][:, :n], WI[:, k, 128:256], H[:, k, c0:c0 + n], k == 0, k == NT - 1,
                           WSK[0:3] + [hk(k, ci)], [("PS", b2)])
                    act(Q[:, c0:c0 + n], ps[b1][:, :n], AF.Identity, [("PS", b1)], [("Q", ci)], scale=0.125)
                    cp("dve", K[:, c0:c0 + n], ps[b2][:, :n], [("PS", b2)], [("K", ci)])

                def evacB(g4, nb, b):
                    cp("act", Vb[:, g4 * 4:g4 * 4 + nb, :, 0:64],
                       ps[b][:, :nb * 128].rearrange("p (b h c) -> p b h c", h=2, c=64),
                       [("PS", b)], [("V", g4 * 4 + i) for i in range(nb)])
                v_proj(WI, 256, evacB)
                if hp < 3:
                    load_WIB(hp + 1)
                if hp == 0:
                    memset("pool", Vb[:, :, :, 64:65], 1.0, [("V", kb) for kb in range(NKB)])
                for j in range(NKB):
                    kbs = (na_kbs(j) + [16, 17]) if j < 16 else [16, 17]
                    batches = [kbs[i:i + 4] for i in range(0, len(kbs), 4)]
                    ds = cnts["ot"] % 2
                    cnts["ot"] += 1
                    for hh in range(2):
                        pr = slice(hh * 64, (hh + 1) * 64)
                        a = cnts["acc"] % 21
                        cnts["acc"] += 1
                        bk, off = 4 + a // 7, (a % 7) * 65
                        kacc = ("ACC", bk, off)
                        nk = 0
                        for bt in batches:
                            b = gbank()
                            for i, kb in enumerate(bt):
                                lat = kb < 16
                                mm(ps[b][:, i * 128:(i + 1) * 128], K[pr, kb * 128:(kb + 1) * 128], Q[pr, j * 128:(j + 1) * 128],
                                   True, not lat, [("K", kb // 4), ("Q", j // 4)], [("PS", b)])
                                if lat:
                                    ti = hh * NBT + na_tile(j, kb)
                                    mm(ps[b][:, i * 128:(i + 1) * 128], IDB[:], RB[:, ti * 128:(ti + 1) * 128],
                                       False, True, ["IDB", "RB"], [("PS", b)])
                            es_ = cnts["e"] % 4
                            cnts["e"] += 1
                            w_ = len(bt) * 128
                            act(E[es_][:, :w_], ps[b][:, :w_], AF.Exp, [("PS", b)], [("E", es_)])
                            for i, kb in enumerate(bt):
                                mm(ps[bk][:, off:off + 65], E[es_][:, i * 128:(i + 1) * 128], Vb[:, kb, hh, :],
                                   nk == 0, nk == len(kbs) - 1, [("E", es_), ("V", kb)], [kacc])
                                nk += 1
                        sm = (cnts["sm"] % 2) * 8
                        cnts["sm"] += 1
                        recip(SMF[:, sm:sm + 1], ps[bk][:, off + 64:off + 65], [kacc], [("SMF", sm, 0)])
                        ts("dve", OT[ds][:, hh * 64:(hh + 1) * 64], ps[bk][:, off:off + 64], SMF[:, sm:sm + 1], None,
                           ALU.mult, None, [kacc, ("SMF", sm, 0)], [("OT", ds, hh)])
                    transpose_out(ds, j)
                out_proj(4 + hp)

            ada_group(l, 3)
            ada_group(l, 4)
            ada_scale(l, 1)
            ada_group(l, 5)
            norm_mod(l, 1, range(5), lambda ci: CH[ci][0])
            mlp(l, range(5), lambda ci: CH[ci][0])

        def layer1():
            l = 1
            MAGIC = 12582912.0
            hcol = lambda ci: (256 + CH[ci][0]) if ci < 4 else 0
            barrier()
            ada_group(l, 0)
            ada_group(l, 1)
            ada_scale(l, 0)
            norm_mod(l, 0, range(5), hcol, dup_ctx_col=2304)
            ada_group(l, 2)
            ada_group(l, 3)
            ada_group(l, 4)
            ada_scale(l, 1)
            ada_group(l, 5)
            if _STOP == "a":
                return

            o = 0
            PR = [arf(o + d * 96, 96) for d in range(2)]; o += 192
            RHO = [arf(o + d * 32, 32) for d in range(2)]; o += 64
            TH = [arf(o + d * 32, 32) for d in range(2)]; o += 64
            tmp_off = o; o += 32 * 24
            CTAB = [arb(o + d * 512, 1024) for d in range(2)]; o += 1024
            BB = [arb(o + d * 512, 1024) for d in range(2)]; o += 1024
            IOTA = arf(o, 512); o += 512
            TAB = [[arf(o + s * 1024, 512), arf(o + s * 1024 + 512, 512)] for s in range(2)]; o += 2048
            o_ys = o
            YS = [arf(o + i * 512, 512) for i in range(3)]; o += 1536
            PP = [arf(o + i * 512, 512) for i in range(2)]; o += 1024
            o_mm = o
            MM = [arf(o + i * 512, 512) for i in range(4)]; o += 2048
            CRI = [arf(o + i * 512, 512) for i in range(2)]; o += 1024
            VRI = [arf(o + i * 512, 512) for i in range(2)]; o += 1024
            ZZ = PP
            CAR = arf(o, 16); o += 16
            BRAW = [arf(o_mm + d * 1024, 1024) for d in range(2)]
            CRAW = [arf(o_ys + d * 1024, 1024) for d in range(2)]
            BLs = [WS[:, 0:2048], WS[:, 2048:4096]]
            CLs = [WS[:, 4096:6144], WS[:, 6144:8192]]
            WADb = WAD[:].rearrange("p a t c -> p (a t c)").bitcast(BF16)
            SRI = [WADb[:, i * 512:(i + 1) * 512] for i in range(4)]
            TIN = WADb[:, 2048:4096]

            P.dma(IOTA, iota_d, w=["IOTA"])
            for d in range(2):
                P.dma(PR[d], s5p_d[d], w=[("PR", d)])
                P.dma(BRAW[d], s5b_d[d], w=[("BRAW", d)])
                P.dma(CRAW[d], s5c_d[d], w=[("CRAW", d)])

            for d in range(2):
                tcount = {"n": 0}

                def tmp():
                    i = tcount["n"]
                    tcount["n"] += 1
                    assert i < 24
                    return arf(tmp_off + i * 32, 32), ("S5T", i)

                LR, LI, LS = PR[d][:, 0:32], PR[d][:, 32:64], PR[d][:, 64:96]
                pk = ("PR", d)
                DT, kDT = tmp()
                act(DT, LS, AF.Exp, [pk], [kDT])
                LRD, kLRD = tmp()
                tt("dve", LRD, LR, DT, ALU.mult, [pk, kDT], [kLRD])
                act(RHO[d], LRD, AF.Exp, [kLRD], [("RHO", d)])
                ANG, kANG = tmp()
                tt("dve", ANG, LI, DT, ALU.mult, [pk, kDT], [kANG])
                Y, kY = tmp()
                ts("dve", Y, ANG, 1.0 / TWO_PI, None, ALU.mult, None, [kANG], [kY])
                KY, kKY = tmp()
                ts("dve", KY, Y, MAGIC, MAGIC, ALU.add, ALU.subtract, [kY], [kKY])
                tt("dve", TH[d], Y, KY, ALU.subtract, [kY, kKY], [("TH", d)])
                SINA, kS = tmp()
                act(SINA, TH[d], AF.Sin, [("TH", d)], [kS], scale=TWO_PI)
                Y2, kY2 = tmp()
                ts("dve", Y2, TH[d], 0.25, None, ALU.add, None, [("TH", d)], [kY2])
                KY2, kKY2 = tmp()
                ts("dve", KY2, Y2, MAGIC, MAGIC, ALU.add, ALU.subtract, [kY2], [kKY2])
                FR2, kF2 = tmp()
                tt("dve", FR2, Y2, KY2, ALU.subtract, [kY2, kKY2], [kF2])
                COSA, kC = tmp()
                act(COSA, FR2, AF.Sin, [kF2], [kC], scale=TWO_PI)
                AR_, kAR = tmp()
                tt("dve", AR_, RHO[d], COSA, ALU.mult, [("RHO", d), kC], [kAR])
                AI_, kAI = tmp()
                tt("dve", AI_, RHO[d], SINA, ALU.mult, [("RHO", d), kS], [kAI])
                NR, kNR = tmp()
                ts("dve", NR, AR_, -1.0, None, ALU.add, None, [kAR], [kNR])
                T_, kT = tmp()
                tt("dve", T_, LR, LR, ALU.mult, [pk], [kT])
                DEN, kDEN = tmp()
                tt("dve", DEN, LI, LI, ALU.mult, [pk], [kDEN])
                tt("dve", DEN, DEN, T_, ALU.add, [kDEN, kT], [kDEN])
                recip(DEN, DEN, [kDEN], [kDEN])
                C1, k1 = tmp()
                tt("dve", C1, NR, LR, ALU.mult, [kNR, pk], [k1])
                C2, k2 = tmp()
                tt("dve", C2, AI_, LI, ALU.mult, [kAI, pk], [k2])
                tt("dve", C1, C1, C2, ALU.add, [k1, k2], [k1])
                CRv, kCR = tmp()
                tt("dve", CRv, C1, DEN, ALU.mult, [k1, kDEN], [kCR])
                C3, k3 = tmp()
                tt("dve", C3, AI_, LR, ALU.mult, [kAI, pk], [k3])
                C4, k4 = tmp()
                tt("dve", C4, NR, LI, ALU.mult, [kNR, pk], [k4])
                tt("dve", C3, C3, C4, ALU.subtract, [k3, k4], [k3])
                CIv, kCI = tmp()
                tt("dve", CIv, C3, DEN, ALU.mult, [k3, kDEN], [kCI])
                BR3 = BRAW[d][:, 0:512].rearrange("p (s h) -> p s h", h=16)
                BI3 = BRAW[d][:, 512:1024].rearrange("p (s h) -> p s h", h=16)
                U = [x.rearrange("p (s h) -> p s h", h=16) for x in (VRI[0], VRI[1], CRI[0], CRI[1])]
                crb = bc_last(CRv, 16)
                cib = bc_last(CIv, 16)
                bk_ = ("BRAW", d)
                tt("dve", U[0], BR3, crb, ALU.mult, [bk_, kCR], ["U0"])
                tt("dve", U[1], BI3, cib, ALU.mult, [bk_, kCI], ["U1"])
                tt("dve", BB[d][:, 0:512].rearrange("p (s h) -> p s h", h=16), U[0], U[1], ALU.subtract, ["U0", "U1"], [("BB", d)])
                tt("dve", U[2], BI3, crb, ALU.mult, [bk_, kCR], ["U2"])
                tt("dve", U[3], BR3, cib, ALU.mult, [bk_, kCI], ["U3"])
                tt("dve", BB[d][:, 512:1024].rearrange("p (s h) -> p s h", h=16), U[2], U[3], ALU.add, ["U2", "U3"], [("BB", d)])
                cp("dve", CTAB[d][:, 0:512], CRAW[d][:, 0:512], [("CRAW", d)], [("CTAB", d)])
                ts("dve", CTAB[d][:, 512:1024], CRAW[d][:, 512:1024], -1.0, None, ALU.mult, None, [("CRAW", d)], [("CTAB", d)])
            barrier()
            memset("pool", TIN, 0.0, [("TIN", 0), ("TIN", 1)])
            memset("pool", CLs[0], 0.0, [("CL", 0, i) for i in range(8)])
            memset("pool", CLs[1], 0.0, [("CL", 1, i) for i in range(8)])
            if _STOP == "b":
                return

            cn = {"tab": 0, "sr": 0, "z": 0, "pt": 0, "car": 0}
            def blk(ap2d, base, n4, step4):
                a = ap2d.ap
                return bass.AP(ap2d.tensor, ap2d.offset + base * a[1][0], [list(a[0]), [step4 * a[1][0], n4], [a[1][0], 16]])

            def build_tables(ct, bs):
                BLc, CLc = BLs[bs], CLs[bs]
                for dr in range(2):
                    for ri in range(2):
                        for gl in range(2):
                            rows = slice(gl * 64, (gl + 1) * 64)
                            src_b = blk(BB[dr][rows, :], ri * 512 + 4 * ct * 16, 4, 16)
                            src_c = blk(CTAB[dr][rows, :], ri * 512 + 4 * ct * 16, 4, 16)
                            dst_t = blk(TIN[rows, :], dr * 1024 + ri * 128 + gl * 16, 4, 288)
                            dst_c = blk(CLc[rows, :], dr * 1024 + ri * 128 + gl * 16, 4, 288)
                            cp("dve", dst_t, src_b, [("BB", dr)], [("TIN", dr)])
                            cp("dve", dst_c, src_c, [("CTAB", dr)], [("CL", bs, dr * 4 + s_) for s_ in range(4)])
                    for g4 in range(2):
                        for q in range(4):
                            ti = dr * 8 + g4 * 4 + q
                            tr(pT[:, q * 128:(q + 1) * 128], TIN[:, ti * 128:(ti + 1) * 128], IDB[:],
                               [("TIN", dr), "IDB"], [("pT", q)])
                        c0 = (dr * 8 + g4 * 4) * 128
                        cp("act", BLc[:, c0:c0 + 512], pT[:, 0:512], [("pT", q) for q in range(4)],
                           [("BL", bs, dr * 4 + g4 * 2), ("BL", bs, dr * 4 + g4 * 2 + 1)])

            cts = range(int(_STOP[1:])) if (_STOP and _STOP[0] == "e" and _STOP[1:].isdigit()) else range(NT)
            if _STOP and _STOP[0] == "x":
                cts = tuple(int(ch) for ch in _STOP[1:])
            for ci_, ct in enumerate(cts):
                bs = ci_ % 2
                build_tables(ct, bs)
                P.dma(s5tab[ct, :, 0:2048], BLs[bs], r=[("BL", bs, i) for i in range(8)], w=[("SCR", ct, 0)])
                P.dma(s5tab[ct, :, 2048:4096], CLs[bs], r=[("CL", bs, i) for i in range(8)], w=[("SCR", ct, 1)])
            barrier()

            if _STOP == "pa":
                return

            def load_tables(ct, bs):
                P.dma(BLs[bs], s5tab[ct, :, 0:2048], r=[("SCR", ct, 0)], w=[("BL", bs, i) for i in range(8)])
                P.dma(CLs[bs], s5tab[ct, :, 2048:4096], r=[("SCR", ct, 1)], w=[("CL", bs, i) for i in range(8)])

            load_tables(cts[0], 0)
            for ci_, ct in enumerate(cts):
                bs = ci_ % 2
                BLc, CLc = BLs[bs], CLs[bs]
                if ci_ + 1 < len(cts):
                    load_tables(cts[ci_ + 1], 1 - bs)
                for dr in range(2):
                    if dr == 0:
                        seq = [(0, 256, None, 4)] + [(256 + c * 512, 512, c, c) for c in range(4)]
                    else:
                        seq = [(2304, 256, None, 5)] + [(256 + c * 512, 512, c, c) for c in (3, 2, 1, 0)]
                    for stl in range(4):
                        st = 4 * ct + stl
                        idx = dr * 4 + stl
                        slot = cn["tab"] % 2
                        cn["tab"] += 1
                        COSt, SINt = TAB[slot]
                        kt = ("TAB", slot)
                        ts("dve", YS[0], IOTA, TH[dr][:, st:st + 1], None, ALU.mult, None, ["IOTA", ("TH", dr)], ["YS0"])
                        ts("dve", YS[1], YS[0], MAGIC, MAGIC, ALU.add, ALU.subtract, ["YS0"], ["YS1"])
                        tt("dve", YS[0], YS[0], YS[1], ALU.subtract, ["YS0", "YS1"], ["YS0"])
                        act(SINt, YS[0], AF.Sin, ["YS0"], [(kt, 1)], scale=TWO_PI)
                        ts("dve", YS[2], YS[0], 0.25, None, ALU.add, None, ["YS0"], ["YS2"])
                        ts("dve", YS[1], YS[2], MAGIC, MAGIC, ALU.add, ALU.subtract, ["YS2"], ["YS1"])
                        tt("dve", YS[2], YS[2], YS[1], ALU.subtract, ["YS2", "YS1"], ["YS2"])
                        act(COSt, YS[2], AF.Sin, ["YS2"], [(kt, 0)], scale=TWO_PI)
                        tabk = [(kt, 0), (kt, 1)]
                        if _STOP == "d1":
                            return
                        rho = RHO[dr][:, st:st + 1]
                        for i, (c0, n, yc, hkc) in enumerate(seq):
                            b_r = gbank((0, 1, 2))
                            b_i = gbank((0, 1, 2))
                            mm(ps[b_r][:, :n], BLc[:, (idx * 2) * 128:(idx * 2 + 1) * 128], H[:, ct, c0:c0 + n], True, True,
                               [("BL", bs, idx), hk(ct, hkc)], [("PS", b_r)])
                            mm(ps[b_i][:, :n], BLc[:, (idx * 2 + 1) * 128:(idx * 2 + 2) * 128], H[:, ct, c0:c0 + n], True, True,
                               [("BL", bs, idx), hk(ct, hkc)], [("PS", b_i)])
                            if dr == 0:
                                cosv, sinv = COSt[:, :n], SINt[:, :n]
                                fw = lambda a: a
                            else:
                                cosv, sinv = rev(COSt[:, :n]), rev(SINt[:, :n])
                                fw = rev
                            tt("dve", MM[0][:, :n], ps[b_r][:, :n], cosv, ALU.mult, [("PS", b_r)] + tabk, ["MM0"])
                            tt("dve", MM[1][:, :n], ps[b_i][:, :n], sinv, ALU.mult, [("PS", b_i)] + tabk, ["MM1"])
                            tt("dve", CRI[0][:, :n], MM[0][:, :n], MM[1][:, :n], ALU.add, ["MM0", "MM1"], ["CRI0"])
                            tt("dve", MM[2][:, :n], ps[b_i][:, :n], cosv, ALU.mult, [("PS", b_i)] + tabk, ["MM2"])
                            tt("dve", MM[3][:, :n], ps[b_r][:, :n], sinv, ALU.mult, [("PS", b_r)] + tabk, ["MM3"])
                            tt("dve", CRI[1][:, :n], MM[2][:, :n], MM[3][:, :n], ALU.subtract, ["MM2", "MM3"], ["CRI1"])
                            if i == 0:
                                ini_r, ini_i, ik = 0.0, 0.0, []
                            else:
                                cs = (cn["car"] % 2) * 2
                                ini_r, ini_i, ik = CAR[:, cs:cs + 1], CAR[:, cs + 1:cs + 2], [("CAR", cs)]
                            scan(fw(VRI[0][:, :n]), rho.to_broadcast([128, n]), fw(CRI[0][:, :n]), ini_r,
                                 ["CRI0", ("RHO", dr)] + ik, ["VRI0"])
                            scan(fw(VRI[1][:, :n]), rho.to_broadcast([128, n]), fw(CRI[1][:, :n]), ini_i,
                                 ["CRI1", ("RHO", dr)] + ik, ["VRI1"])
                            if i < len(seq) - 1:
                                last = n - 1 if dr == 0 else 0
                                cn["car"] += 1
                                cs = (cn["car"] % 2) * 2
                                cN, sN = COSt[:, n - 1:n], SINt[:, n - 1:n]
                                vrl, vil = VRI[0][:, last:last + 1], VRI[1][:, last:last + 1]
                                tt("dve", CAR[:, 8:9], vil, sN, ALU.mult, ["VRI1"] + tabk, ["CAR8"])
                                stt(CAR[:, cs:cs + 1], vrl, cN, CAR[:, 8:9], ALU.mult, ALU.subtract, ["VRI0", "CAR8"] + tabk, [("CAR", cs)])
                                tt("dve", CAR[:, 9:10], vil, cN, ALU.mult, ["VRI1"] + tabk, ["CAR9"])
                                stt(CAR[:, cs + 1:cs + 2], vrl, sN, CAR[:, 9:10], ALU.mult, ALU.add, ["VRI0", "CAR9"] + tabk, [("CAR", cs)])
                            if yc is not None:
                                srs = (cn["sr"] % 2) * 2
                                cn["sr"] += 1
                                tt("dve", PP[0][:, :n], VRI[0][:, :n], cosv, ALU.mult, ["VRI0"] + tabk, ["PP0"])
                                tt("dve", PP[1][:, :n], VRI[1][:, :n], sinv, ALU.mult, ["VRI1"] + tabk, ["PP1"])
                                tt("dve", SRI[srs][:, :n], PP[0][:, :n], PP[1][:, :n], ALU.subtract, ["PP0", "PP1"], [("SRI", srs)])
                                tt("dve", PP[0][:, :n], VRI[0][:, :n], sinv, ALU.mult, ["VRI0"] + tabk, ["PP0"])
                                tt("dve", PP[1][:, :n], VRI[1][:, :n], cosv, ALU.mult, ["VRI1"] + tabk, ["PP1"])
                                tt("dve", SRI[srs + 1][:, :n], PP[0][:, :n], PP[1][:, :n], ALU.add, ["PP0", "PP1"], [("SRI", srs + 1)])
                                first = (dr == 0 and stl == 0)
                                lastm = (dr == 1 and stl == 3)
                                mm(ps[3 + yc][:, :n], CLc[:, (idx * 2) * 128:(idx * 2 + 1) * 128], SRI[srs][:, :n], first, False,
                                   [("CL", bs, idx), ("SRI", srs)], [("PS", 3 + yc)])
                                mm(ps[3 + yc][:, :n], CLc[:, (idx * 2 + 1) * 128:(idx * 2 + 2) * 128], SRI[srs + 1][:, :n], False, lastm,
                                   [("CL", bs, idx), ("SRI", srs + 1)], [("PS", 3 + yc)])
                        if _STOP == "d":
                            return
                    if _STOP == "d2":
                        return
                for yc in range(4):
                    c0h = 256 + yc * 512
                    z = cn["z"] % 2
                    cn["z"] += 1
                    stt(ZZ[z], H[:, ct, c0h:c0h + 512], SSMD[:, ct:ct + 1], ps[3 + yc][:, :], ALU.mult, ALU.add,
                        [hk(ct, yc), "SSMD", ("PS", 3 + yc)], ["PP%d" % z])
                    G1_, G2_ = CRI[z], VRI[z]
                    k1_, k2_ = "CRI%d" % z, "VRI%d" % z
                    tt("dve", G1_, ZZ[z], ZZ[z], ALU.mult, ["PP%d" % z], [k1_])
                    ts("dve", G1_, G1_, 0.044715, 1.0, ALU.mult, ALU.add, [k1_], [k1_])
                    tt("dve", G1_, G1_, ZZ[z], ALU.mult, [k1_, "PP%d" % z], [k1_])
                    act(G2_, G1_, AF.Sigmoid, [k1_], [k2_], scale=1.5957691216)
                    tt("dve", H[:, ct, c0h:c0h + 512], ZZ[z], G2_, ALU.mult, ["PP%d" % z, k2_], [hk(ct, yc)])

            barrier()
            if _STOP and _STOP[0] in "ex":
                return

            def load_glu(dt):
                s = dt % 2
                P.dma(WS[:, s * 4096: s * 4096 + 1024].rearrange("p (t c) -> p t c", t=NT),
                      glu_a[:, dt * 128:(dt + 1) * 128].rearrange("(t p) c -> p t c", p=128), w=[("WS", s, 0)], eng="pool")
                P.dma(WS[:, s * 4096 + 1024: s * 4096 + 2048].rearrange("p (t c) -> p t c", t=NT),
                      glu_b[:, dt * 128:(dt + 1) * 128].rearrange("(t p) c -> p t c", p=128), w=[("WS", s, 0)], eng="pool")

            load_glu(0)
            for dt in range(NT):
                s = dt % 2
                GA = WS[:, s * 4096: s * 4096 + 1024]
                GB = WS[:, s * 4096 + 1024: s * 4096 + 2048]
                if dt + 1 < NT:
                    load_glu(dt + 1)
                for yc in range(4):
                    c0, n = CH[yc]
                    hc = 256 + c0
                    bA = gbank((0, 1, 2, 3, 4, 5, 6))
                    bB = gbank((0, 1, 2, 3, 4, 5, 6))
                    for k in range(NT):
                        mm(ps[bA][:, :n], GA[:, k * 128:(k + 1) * 128], H[:, k, hc:hc + n], k == 0, k == NT - 1,
                           [("WS", s, 0), hk(k, yc)], [("PS", bA)])
                    for k in range(NT):
                        mm(ps[bB][:, :n], GB[:, k * 128:(k + 1) * 128], H[:, k, hc:hc + n], k == 0, k == NT - 1,
                           [("WS", s, 0), hk(k, yc)], [("PS", bB)])
                    z = cn["z"] % 2
                    cn["z"] += 1
                    act(ZZ[z], ps[bB][:, :n], AF.Sigmoid, [("PS", bB)], [("ZZ", z)])
                    tt("dve", VRI[z], ps[bA][:, :n], ZZ[z], ALU.mult, [("PS", bA), ("ZZ", z)], [("GT", z)])
                    stt(X[:, dt, c0:c0 + n], VRI[z], MOD[:, l, 16 + dt, 0:1], X[:, dt, c0:c0 + n], ALU.mult, ALU.add,
                        [("GT", z), ("MOD", l, 2, 0), xk(dt, yc)], [xk(dt, yc)])
            if _STOP == "f":
                return

            norm_mod(l, 1, range(4), hcol)
            mlp(l, range(4), hcol)

        out_ops = []
        if n_layers >= 1:
            layer0()
        if n_layers >= 2:
            layer1()
        if dbg:
            for t in range(NT):
                out_ops.append(P.dma(outT[t * 128:(t + 1) * 128, :], X[:, t, :], r=[xk(t, c) for c in range(5)]))
        else:
            OS = [AR[:, i * 512:(i + 1) * 512] for i in range(4)]
            barrier()
            ocnt = 0
            for ci in range(4):
                c0, n = CH[ci]
                rs = ci % 2
                rms_stats(ci, rs)
                for t in range(NT):
                    s = ncnt["n"] % 2
                    ncnt["n"] += 1
                    tt("dve", TMf[:, s * 512:s * 512 + n], X[:, t, c0:c0 + n], RSf[:, rs * 512:rs * 512 + n], ALU.mult,
                       [xk(t, ci), ("RS", rs)], [("TM", s)])
                    osl = ocnt % 4
                    ocnt += 1
                    act(OS[osl], TMf[:, s * 512:s * 512 + n], AF.Identity, [("TM", s), "NG"], [("OS", osl)],
                        scale=NG[:, 32 + t:33 + t])
                    out_ops.append(P.dma(outT[t * 128:(t + 1) * 128, c0:c0 + n], OS[osl], r=[("OS", osl)]))
        P.emit(final_wait_ops=out_ops)
    return nc


def _rope_tables():
    t = np.arange(NLAT)
    row = (t // 64).astype(np.float32)
    col = (t % 64).astype(np.float32)
    inv_freq = (10000.0 ** (-np.arange(16, dtype=np.float32) / 16)).astype(np.float32)
    ang_r = row[:, None] * inv_freq[None, :]
    ang_c = col[:, None] * inv_freq[None, :]
    cos = np.ones((128, NTOK), np.float32)
    sin = np.zeros((128, NTOK), np.float32)
    for p in range(128):
        d = p % 64
        f = d % 16
        ang = ang_r[:, f] if d < 32 else ang_c[:, f]
        sgn = -1.0 if (d % 32) < 16 else 1.0
        cos[p, :NLAT] = np.cos(ang)
        sin[p, :NLAT] = sgn * np.sin(ang)
    return np.concatenate([cos, sin], axis=1).astype(np.float32)


def _swap_idx():
    idx = np.arange(128)
    d = idx % 64
    partner = np.where((d % 32) < 16, d + 16, d - 16)
    return (idx // 64) * 64 + partner


def _bias_tables(rpb):
    out = np.full((8, NBT, 128, 128), -30000.0, np.float32)
    kl = np.arange(128)[:, None]
    ql = np.arange(128)[None, :]
    combos = [(5, 5 + dlt, dlt + 2) for dlt in range(-2, 3)]
    combos += [(j, kb, na_tile(j, kb)) for j in (0, 1, 14, 15) for kb in na_kbs(j)]
    for (j, kb, ti) in combos:
        qrow = 2 * j + ql // 64
        qcol = ql % 64
        krow = 2 * kb + kl // 64
        kcol = kl % 64
        rs = np.clip(qrow - 4, 0, 24)
        cs = np.clip(qcol - 8, 0, 48)
        valid = (krow >= rs) & (krow < rs + 8) & (kcol >= cs) & (kcol < cs + 16)
        ro = np.clip(krow - qrow + 7, 0, 14)
        co = np.clip(kcol - qcol + 15, 0, 30)
        for h in range(8):
            out[h, ti] = np.where(valid, rpb[h][ro, co], np.float32(-30000.0))
    return out


def _prep(inp):
    f = np.float32
    x = np.asarray(inp["x"], f)
    ctx = np.asarray(inp["ctx"], f)
    c = np.asarray(inp["c"], f)
    c_ctx = np.asarray(inp["c_ctx"], f)
    shared = {}
    shared["w_ada"] = np.ascontiguousarray(inp["w_ada"], f)
    shared["b_ada"] = np.ascontiguousarray(np.asarray(inp["b_ada"], f).reshape(2, 48, 128).transpose(2, 0, 1).reshape(128, 96))
    ng = np.concatenate([np.asarray(inp["norm1_g"], f).reshape(2, 8, 128), np.asarray(inp["norm2_g"], f).reshape(2, 8, 128),
                         np.asarray(inp["final_g"], f).reshape(1, 8, 128)], axis=0)
    shared["ng"] = np.ascontiguousarray(ng.transpose(2, 0, 1).reshape(128, 40))
    w_in = np.asarray(inp["w_in"], f)[0]
    sw = _swap_idx()
    wA = np.empty((4, D, 640), f)
    wB = np.empty((4, D, 384), f)
    for h in range(4):
        q = w_in[:, h * 128:(h + 1) * 128]
        k = w_in[:, 512 + h * 128: 512 + (h + 1) * 128]
        v = w_in[:, 1024 + h * 128: 1024 + (h + 1) * 128]
        wA[h] = np.concatenate([q, q[:, sw], k, k[:, sw], v], axis=1)
        wB[h] = np.concatenate([w_in[:, 1536 + h * 128:1536 + (h + 1) * 128], w_in[:, 2048 + h * 128:2048 + (h + 1) * 128],
                                w_in[:, 2560 + h * 128:2560 + (h + 1) * 128]], axis=1)
    shared["w_inA"] = wA
    shared["w_inB"] = wB
    shared["w_out"] = np.ascontiguousarray(np.asarray(inp["w_out"], f)[0])
    lamv = np.concatenate([np.asarray(inp[k], f)[0] for k in ("lam_q1", "lam_k1", "lam_q2", "lam_k2")])
    shared["lamv"] = np.ascontiguousarray(np.broadcast_to(lamv[None, :], (128, 256)))
    shared["subg"] = np.ascontiguousarray(np.broadcast_to(np.asarray(inp["subln_g"], f)[0][None, :], (128, 128)))
    bt = _bias_tables(np.asarray(inp["na_rpb"], f)[0])
    bt = bt.reshape(4, 2, NBT, 128, 128).transpose(0, 3, 1, 2, 4).reshape(4, 128, 2 * NBT * 128)
    shared["biasT"] = np.ascontiguousarray(bt)
    shared["rope"] = _rope_tables()
    shared["ident"] = np.eye(128, dtype=f)
    shared["iota1"] = np.ascontiguousarray(np.broadcast_to(np.arange(1, 513, dtype=f)[None, :], (128, 512)))

    def st_layout(a):
        sh = a.shape
        a = a.reshape((2, 32, 2, 64) + sh[3:])
        perm = (0, 2, 3, 1) + tuple(range(4, a.ndim))
        a = a.transpose(perm)
        return a.reshape((2, 128, 32) + sh[3:])
    lre = st_layout(np.asarray(inp["ssm_lam_re"], f)[0])
    lim = st_layout(np.asarray(inp["ssm_lam_im"], f)[0])
    lst = st_layout(np.ascontiguousarray(np.broadcast_to(np.asarray(inp["ssm_log_step"], f)[0][:, :, None], (2, 64, 64))))
    shared["s5p"] = np.ascontiguousarray(np.concatenate([lre, lim, lst], axis=2))
    bre = st_layout(np.asarray(inp["ssm_b_re"], f)[0]).reshape(2, 128, 512)
    bim = st_layout(np.asarray(inp["ssm_b_im"], f)[0]).reshape(2, 128, 512)
    shared["s5b"] = np.ascontiguousarray(np.concatenate([bre, bim], axis=2))
    cre = st_layout(np.asarray(inp["ssm_c_re"], f)[0].transpose(0, 1, 3, 2)).reshape(2, 128, 512)
    cim = st_layout(np.asarray(inp["ssm_c_im"], f)[0].transpose(0, 1, 3, 2)).reshape(2, 128, 512)
    shared["s5c"] = np.ascontiguousarray(np.concatenate([cre, cim], axis=2))
    shared["ssmd"] = np.ascontiguousarray(np.asarray(inp["ssm_d"], f)[0].reshape(8, 128).T)
    shared["glu_w_a"] = np.ascontiguousarray(np.asarray(inp["glu_w_a"], f)[0])
    shared["glu_w_b"] = np.ascontiguousarray(np.asarray(inp["glu_w_b"], f)[0])
    shared["mlp_w1"] = np.ascontiguousarray(inp["mlp_w1"], f)
    shared["mlp_w2"] = np.ascontiguousarray(inp["mlp_w2"], f)
    maps = []
    for b in range(8):
        m = dict(shared)
        m["xT"] = np.ascontiguousarray(np.concatenate([x[b].T, ctx[b].T], axis=1))
        cc = np.stack([c[b].reshape(8, 128).T, c_ctx.reshape(8, 128).T], axis=2)
        m["cc"] = np.ascontiguousarray(cc.reshape(128, 16))
        maps.append(m)
    return maps


_NC_CACHE = {}


def kernel(**inputs):
    maps = _prep(inputs)
    if "nc" not in _NC_CACHE:
        _NC_CACHE["nc"] = build()
    res = run_bass_kernel_spmd(_NC_CACHE["nc"], maps, core_ids=list(range(8)))
    out = np.stack([np.ascontiguousarray(r["outT"].T) for r in res.results], axis=0)
    return out.astype(np.float32)
```
